# Optimizing a Trainium2 kernel written in Bass

```python
import jax, jax.numpy as jnp
from jax import lax
import numpy as np

D_MODEL = 2048
BATCH = 4
SEQ = 8192
DEPTH = 1

D_RNN = 2048
N_RNN_BLOCKS = 8
RNN_BLOCK = D_RNN // N_RNN_BLOCKS
CONV_WIDTH = 4
LRU_C = 8.0
N_Q_HEADS = 32
N_KV_HEADS = 4
HEAD_DIM = 64
Q_GROUP = N_Q_HEADS // N_KV_HEADS
WINDOW = 128
ATTN_BLOCK = 128
ROPE_THETA = 10000.0
D_FF = 4 * D_MODEL
RMS_EPS = 1e-6
NEG_INF = -1e30

IN_SPLITS = (D_RNN, D_RNN, N_Q_HEADS * HEAD_DIM, N_KV_HEADS * HEAD_DIM, N_KV_HEADS * HEAD_DIM, D_MODEL, D_MODEL)
IN_OFFSETS = tuple(int(o) for o in np.cumsum(IN_SPLITS)[:-1])
D_IN = int(sum(IN_SPLITS))

kernel_name = "hybrid_rglru_swa_sink_gated_block"


def rmsnorm(x, g):
    xf = x.astype(jnp.float32)
    y = xf * lax.rsqrt(jnp.mean(xf * xf, axis=-1, keepdims=True) + RMS_EPS)
    return (y * g.astype(jnp.float32)).astype(x.dtype)


def rope(x, positions):
    half = HEAD_DIM // 2
    inv_freq = ROPE_THETA ** (-jnp.arange(0, HEAD_DIM, 2, dtype=jnp.float32) / HEAD_DIM)
    ang = positions.astype(jnp.float32)[..., None] * inv_freq
    cos = jnp.cos(ang)[:, :, None, :]
    sin = jnp.sin(ang)[:, :, None, :]
    xf = x.astype(jnp.float32)
    x1, x2 = xf[..., :half], xf[..., half:]
    out = jnp.concatenate([x1 * cos - x2 * sin, x2 * cos + x1 * sin], axis=-1)
    return out.astype(x.dtype)


def causal_depthwise_conv(x, w, b):
    S = x.shape[1]
    xp = jnp.pad(x, ((0, 0), (CONV_WIDTH - 1, 0), (0, 0)))
    y = b
    for k in range(CONV_WIDTH):
        y = y + xp[:, k:k + S] * w[k]
    return y


def rg_lru(x, positions, w_a, b_a, w_x, b_x, lam):
    B, S, _ = x.shape
    xf = x.astype(jnp.float32)
    xb = xf.reshape(B, S, N_RNN_BLOCKS, RNN_BLOCK)
    r = jax.nn.sigmoid(jnp.einsum('bsni,nij->bsnj', xb, w_a.astype(jnp.float32)).reshape(B, S, D_RNN) + b_a)
    i = jax.nn.sigmoid(jnp.einsum('bsni,nij->bsnj', xb, w_x.astype(jnp.float32)).reshape(B, S, D_RNN) + b_x)
    log_a = -LRU_C * r * jax.nn.softplus(-lam.astype(jnp.float32))
    a = jnp.exp(log_a)
    mult = jnp.sqrt(-jnp.expm1(2.0 * log_a))
    reset = (positions == 0)[..., None]
    a = jnp.where(reset, 0.0, a)
    mult = jnp.where(reset, 1.0, mult)
    b = mult * (i * xf)

    def combine(left, right):
        a_l, b_l = left
        a_r, b_r = right
        return a_l * a_r, a_r * b_l + b_r

    _, h = lax.associative_scan(combine, (a, b), axis=1)
    return h.astype(x.dtype)


def sliding_window_sink_attention(q, k, v, sinks):
    B, S = q.shape[0], q.shape[1]
    T = ATTN_BLOCK
    NB = S // T
    scale = HEAD_DIM ** -0.5
    qb = q.reshape(B, NB, T, N_KV_HEADS, Q_GROUP, HEAD_DIM) * scale
    kb = k.reshape(B, NB, T, N_KV_HEADS, HEAD_DIM)
    vb = v.reshape(B, NB, T, N_KV_HEADS, HEAD_DIM)
    pad = ((0, 0), (1, 0), (0, 0), (0, 0), (0, 0))
    kk = jnp.concatenate([jnp.pad(kb, pad)[:, :-1], kb], axis=2)
    vv = jnp.concatenate([jnp.pad(vb, pad)[:, :-1], vb], axis=2)
    scores = jnp.einsum('bnqhgd,bnkhd->bnhgqk', qb, kk).astype(jnp.float32)
    qi = jnp.arange(T)[:, None]
    ki = jnp.arange(2 * T)[None, :]
    dist = qi + T - ki
    band = (dist >= 0) & (dist < WINDOW)
    not_pad = (jnp.arange(NB)[:, None, None] > 0) | (ki >= T)[None]
    valid = (band[None] & not_pad)[None, :, None, None]
    scores = jnp.where(valid, scores, NEG_INF)
    sink = sinks.astype(jnp.float32).reshape(N_KV_HEADS, Q_GROUP)[None, None, :, :, None, None]
    m = jnp.maximum(jnp.max(scores, axis=-1, keepdims=True), sink)
    p = jnp.exp(scores - m)
    denom = jnp.sum(p, axis=-1, keepdims=True) + jnp.exp(sink - m)
    probs = (p / denom).astype(v.dtype)
    out = jnp.einsum('bnhgqk,bnkhd->bnqhgd', probs, vv)
    return out.reshape(B, S, N_Q_HEADS * HEAD_DIM)


def setup_inputs(seed: int = 0) -> dict:
    key = jax.random.key(seed)
    ks = jax.random.split(key, 20)
    f32 = jnp.float32
    L = DEPTH

    def nrm(k, shape, scale):
        return jax.random.normal(k, shape, f32) * scale

    def gain(k):
        return 1.0 + 0.05 * jax.random.normal(k, (L, D_MODEL), f32)

    x = jax.random.normal(ks[0], (BATCH, SEQ, D_MODEL), f32)
    positions = jnp.broadcast_to(jnp.arange(SEQ, dtype=jnp.int32)[None, :], (BATCH, SEQ))
    u = jax.random.uniform(ks[9], (L, D_RNN), f32, 0.9, 0.999)
    a0 = u ** (1.0 / LRU_C)
    lru_lambda = jnp.log(a0) - jnp.log1p(-a0)
    return {
        "x": x,
        "positions": positions,
        "norm_mix_pre": gain(ks[1]),
        "w_in": nrm(ks[2], (L, D_MODEL, D_IN), D_MODEL ** -0.5),
        "conv_w": nrm(ks[3], (L, CONV_WIDTH, D_RNN), CONV_WIDTH ** -0.5),
        "conv_b": nrm(ks[4], (L, D_RNN), 0.01),
        "w_rg_a": nrm(ks[5], (L, N_RNN_BLOCKS, RNN_BLOCK, RNN_BLOCK), RNN_BLOCK ** -0.5),
        "b_rg_a": nrm(ks[6], (L, D_RNN), 0.01),
        "w_rg_x": nrm(ks[7], (L, N_RNN_BLOCKS, RNN_BLOCK, RNN_BLOCK), RNN_BLOCK ** -0.5),
        "b_rg_x": nrm(ks[8], (L, D_RNN), 0.01),
        "lru_lambda": lru_lambda,
        "attn_sinks": nrm(ks[10], (L, N_Q_HEADS), 0.5),
        "w_rnn_proj": nrm(ks[11], (L, D_RNN, D_MODEL), D_RNN ** -0.5),
        "w_attn_proj": nrm(ks[12], (L, N_Q_HEADS * HEAD_DIM, D_MODEL), (N_Q_HEADS * HEAD_DIM) ** -0.5),
        "w_out": nrm(ks[13], (L, D_MODEL, D_MODEL), D_MODEL ** -0.5),
        "norm_mix_post": gain(ks[14]),
        "norm_mlp_pre": gain(ks[15]),
        "w_mlp_up": nrm(ks[16], (L, D_MODEL, D_FF), D_MODEL ** -0.5),
        "w_mlp_down": nrm(ks[17], (L, D_FF, D_MODEL), D_FF ** -0.5),
        "norm_mlp_post": gain(ks[18]),
    }


def reference(x, positions, norm_mix_pre, w_in, conv_w, conv_b, w_rg_a, b_rg_a, w_rg_x, b_rg_x, lru_lambda, attn_sinks, w_rnn_proj, w_attn_proj, w_out, norm_mix_post, norm_mlp_pre, w_mlp_up, w_mlp_down, norm_mlp_post):
    B, S, _ = x.shape
    for l in range(DEPTH):
        h = rmsnorm(x, norm_mix_pre[l])
        proj = h @ w_in[l]
        xr, yr, q, k, v, g_rnn, g_attn = jnp.split(proj, IN_OFFSETS, axis=-1)
        xr = causal_depthwise_conv(xr, conv_w[l], conv_b[l])
        xr = rg_lru(xr, positions, w_rg_a[l], b_rg_a[l], w_rg_x[l], b_rg_x[l], lru_lambda[l])
        y_rnn = xr * jax.nn.gelu(yr)
        q = rope(q.reshape(B, S, N_Q_HEADS, HEAD_DIM), positions)
        k = rope(k.reshape(B, S, N_KV_HEADS, HEAD_DIM), positions)
        v = v.reshape(B, S, N_KV_HEADS, HEAD_DIM)
        y_att = sliding_window_sink_attention(q, k, v, attn_sinks[l])
        mix = jax.nn.sigmoid(g_rnn) * (y_rnn @ w_rnn_proj[l]) + jax.nn.sigmoid(g_attn) * (y_att @ w_attn_proj[l])
        x = x + rmsnorm(mix @ w_out[l], norm_mix_post[l])
        h = rmsnorm(x, norm_mlp_pre[l])
        u = jnp.square(jax.nn.relu(h @ w_mlp_up[l]))
        x = x + rmsnorm(u @ w_mlp_down[l], norm_mlp_post[l])
    return x
```

```python
import contextlib
from contextlib import ExitStack

import numpy as np
import concourse.bass as bass
import concourse.mybir as mybir
from concourse.bass_utils import run_bass_kernel_spmd

F32 = mybir.dt.float32
BF16 = mybir.dt.bfloat16
I32 = mybir.dt.int32
AF = mybir.ActivationFunctionType
ALU = mybir.AluOpType

D = 2048
KC = 16
T = 512
NW = 6
NCS = 32
EPS = 1e-6
BIGR = 1.0e6
MAGIC = 12582912.0
TWO_PI_S = 6.283185307179586 * (1.0 - 2e-6)

ENGS = ("pe", "act", "dve", "pool", "sp")


class Res:
    __slots__ = ("name", "writer", "readers", "excl")

    def __init__(self, name, excl=False):
        self.name = name
        self.writer = None
        self.readers = []
        self.excl = excl


class Ins:
    __slots__ = ("eng", "fn", "deps", "signal", "tok", "dma", "idx")

    def __init__(self, eng, fn, dma):
        self.eng = eng
        self.fn = fn
        self.deps = []
        self.signal = False
        self.tok = None
        self.dma = dma
        self.idx = -1


class Sched:
    def __init__(self, nc):
        self.nc = nc
        self.streams = {e: [] for e in ENGS}
        self.n = 0
        import os
        self.maxops = int(os.environ.get("KMAXOPS", "100000000"))

    def op(self, eng, fn, reads=(), writes=(), dma=None):
        if self.n >= self.maxops:
            return None
        ins = Ins(eng, fn, dma)
        ins.idx = self.n
        self.n += 1
        deps = {}

        def flat(xs):
            out = []
            for x_ in xs:
                if isinstance(x_, (list, tuple)):
                    out.extend(flat(x_))
                else:
                    out.append(x_)
            return out

        reads = flat(reads)
        writes = flat(writes)

        def add(d, raw):
            if d is None:
                return
            if d.dma is None and dma is None and d.eng == eng:
                if eng == "pe" or not raw:
                    return
            deps[id(d)] = d

        for r in reads:
            add(r.writer, True)
            if r.excl:
                for rd in r.readers:
                    if rd.eng != eng:
                        add(rd, False)
        for w in writes:
            add(w.writer, False)
            for rd in w.readers:
                add(rd, False)
        ins.deps = list(deps.values())
        for d in ins.deps:
            d.signal = True
        for r in reads:
            r.readers.append(ins)
        for w in writes:
            w.writer = ins
            w.readers = []
        self.streams[eng].append(ins)
        return ins

    def fence(self, eng, inss):
        ins = Ins(eng, None, None)
        ins.idx = self.n
        self.n += 1
        ins.deps = [i for i in inss if i is not None]
        for d in ins.deps:
            d.signal = True
        self.streams[eng].append(ins)
        return ins

    def emit(self, stack):
        nc = self.nc
        eng_sem = {e: stack.enter_context(nc.semaphore("S_" + e)) for e in ENGS}
        dma_sem = {}
        dma_cnt = {}
        allins = []
        for e in ENGS:
            allins.extend(self.streams[e])
        allins.sort(key=lambda i: i.idx)
        cnt = {e: 0 for e in ENGS}
        for ins in allins:
            if ins.dma is not None:
                if ins.dma not in dma_sem:
                    dma_sem[ins.dma] = stack.enter_context(nc.semaphore("D_" + ins.dma))
                    dma_cnt[ins.dma] = 0
                dma_cnt[ins.dma] += 16
                ins.tok = (dma_sem[ins.dma], dma_cnt[ins.dma])
            elif ins.signal and ins.fn is not None:
                cnt[ins.eng] += 1
                ins.tok = (eng_sem[ins.eng], cnt[ins.eng])
        self.counts = cnt
        block = stack.enter_context(nc.Block())

        def make_body(e):
            stream = self.streams[e]

            def body(eng):
                waited = {}
                for ins in stream:
                    for d in ins.deps:
                        sem, val = d.tok
                        k = id(sem)
                        if waited.get(k, 0) >= val:
                            continue
                        waited[k] = val
                        eng.wait_ge(sem, val)
                    if ins.fn is None:
                        continue
                    inst = ins.fn(eng)
                    if ins.dma is not None:
                        inst.then_inc(ins.tok[0], 16)
                    elif ins.signal:
                        inst.then_inc(ins.tok[0], 1)

            return body

        block.tensor(make_body("pe"))
        block.scalar(make_body("act"))
        block.vector(make_body("dve"))
        block.gpsimd(make_body("pool"))
        block.sync(make_body("sp"))


PC_GPRE, PC_GPOST, PC_GMPRE, PC_GMPOST = 0, 16, 32, 48
PC_CONVW = 64
PC_CONVB, PC_BA, PC_BX, PC_LAM = 128, 144, 160, 176
PC_SINK = 192
PC_INVF = 224
PC_FLAG = 225
NPP = 228

WSPEC = [
    ("w_xr", 16, 2048), ("w_g", 8, 1024), ("w_kd", 4, 2048), ("w_v", 4, 2048),
    ("w_yr", 16, 2048), ("w_q", 16, 2048), ("w_gr", 16, 2048), ("w_rnn", 16, 2048),
    ("w_ga", 16, 2048), ("w_attn", 16, 2048), ("w_out", 16, 2048), ("w_up", 64, 2048),
    ("w_down", 64, 2048),
]


def build_program(NSB, NPB):
    TOK = NSB * T
    TOKP = NPB * T
    nc = bass.Bass("TRN2", target_bir_lowering=False)
    dt_in = lambda name, shape, dt=F32: nc.dram_tensor(name, shape, dt, kind="ExternalInput").ap()
    xT = dt_in("xT", [D, TOK])
    xpT = dt_in("xpT", [D, TOKP])
    posb = dt_in("posb", [128, TOK], I32)
    pospb = dt_in("pospb", [128, TOKP], I32)
    pp_d = dt_in("pp", [128, NPP])
    cst_d = dt_in("cst", [128, 384])
    yT = nc.dram_tensor("yT", [D, TOK], F32, kind="ExternalOutput").ap()
    wext = {}
    wbf = {}
    for name, nu, cols in WSPEC:
        wext[name] = dt_in(name, [nu, 128, cols])
        wbf[name] = nc.dram_tensor(name + "_bf", [nu, 128, cols], BF16, kind="Internal").ap()

    es = ExitStack()
    with es:
        def sb(name, shape, dt):
            return es.enter_context(nc.sbuf_tensor(name, shape, dt))

        X = sb("X", [128, KC, T], F32)
        H = sb("H", [128, KC, T], BF16)
        O = sb("O", [128, KC, T], F32)
        R = sb("R", [128, 64, T], BF16)
        WR = sb("WR", [128, NW, 2048], BF16)
        PPt = sb("PPt", [128, NPP], F32)
        CST = sb("CST", [128, 384], F32)
        permb = sb("permb", [128, 128], BF16)
        mprev = sb("mprev", [128, 128], BF16)
        mprev0 = sb("mprev0", [128, 128], BF16)
        mown = sb("mown", [128, 128], BF16)
        onesb = sb("onesb", [128, 128], BF16)
        nsp8 = sb("nsp8", [128, 16], F32)
        nsp16 = sb("nsp16", [128, 16], F32)
        esink = sb("esink", [128, 32], F32)
        sm = sb("sm", [128, 8, 16], F32)
        hstate = sb("hstate", [128, 16], F32)
        xhalo = sb("xhalo", [128, 16, 4], F32)
        posi = sb("posi", [128, T], I32)
        posf = sb("posf", [128, T], F32)
        resetb = sb("resetb", [128, T], F32)
        cosT = sb("cosT", [128, T], F32)
        sinT = sb("sinT", [128, T], F32)
        sq = sb("sq", [128, 2, T], BF16)
        rstd = sb("rstd", [128, T], F32)
        XP = sb("XP", [128, 2, 516], F32)
        xcb = sb("xcb", [128, 4, T], BF16)
        Kr = sb("Kr", [128, 4, 640], BF16)
        Vt = sb("Vt", [128, 5, 512], BF16)
        banks = [es.enter_context(nc.psum_tensor(f"pb{i}", [128, T], F32)) for i in range(8)]

        S = Sched(nc)
        Xr = [Res(f"X{c}") for c in range(KC)]
        Hr = [Res(f"H{c}") for c in range(KC)]
        Or = [Res(f"O{c}") for c in range(KC)]
        Rr = [[Res(f"R{j}")] for j in range(64)]
        for c in range(16):
            Rr[16 + c] = [Res(f"QA{c}_{q}") for q in range(4)]
        WRr = [Res(f"W{i}") for i in range(NW)]
        Br = [Res(f"B{i}", excl=True) for i in range(8)]
        r_pp, r_cst, r_perm, r_mprev, r_mprev0, r_mown, r_ones = [Res(n) for n in "pp cst perm mprev mprev0 mown ones".split()]
        r_nsp, r_esink, r_sm = Res("nsp"), Res("esink"), Res("sm")
        hst_r = [Res(f"hst{c}") for c in range(16)]
        xh_r = [Res(f"xh{c}") for c in range(16)]
        r_posi, r_posf, r_resetb, r_cos, r_sin, r_rstd = [Res(n) for n in "posi posf resetb cos sin rstd".split()]
        sq_r = [Res("sq0"), Res("sq1")]
        XP_r = [Res("XP0"), Res("XP1")]
        xcb_r = [Res(f"xcb{i}") for i in range(4)]
        Kr_r = [Res(f"Kr{g}") for g in range(4)]
        Krh_r = [Res(f"Krh{g}") for g in range(4)]
        Vt_r = [Res(f"Vt{b}") for b in range(5)]
        wunit_r = {name: [Res(f"{name}{u}") for u in range(nu)] for name, nu, cols in WSPEC}
        cs_r = [Res(f"cs{i}") for i in range(NCS)]

        st = {"bank": 0, "osc": 0, "w": 0, "pt": 0, "cast": 0}

        def bank():
            n = st.get("nbank", 7)
            i = st["bank"] % n
            st["bank"] = (i + 1) % n
            return banks[i], Br[i]

        def rsc():
            i = st.get("rsc", 0)
            st["rsc"] = (i + 1) % 40
            if i < 16:
                return O[:, i, :], [Or[i]]
            j = 16 + 2 * (i - 16)
            return R[:, j:j + 2, :].rearrange("p a t -> p (a t)").bitcast(F32), Rr[j] + Rr[j + 1]

        SSQ, SSQr = banks[7], Br[7]

        def osc():
            i = st["osc"]
            st["osc"] = (i + 1) % KC
            return O[:, i, :], Or[i]

        def ptile():
            i = st["pt"]
            st["pt"] = (i + 1) % 8
            return R[:, 48 + i, :], Rr[48 + i][0]

        def qtile():
            i = st.get("qt", 0)
            st["qt"] = (i + 1) % 4
            return R[:, 56 + i, :], Rr[56 + i][0]

        def wload(name, u):
            cols = dict((n, c) for n, _, c in WSPEC)[name]
            i = st["w"]
            st["w"] = (i + 1) % NW
            src = wbf[name][u]
            dst = WR[:, i, 0:cols]
            S.op("sp", lambda e: e.dma_start(out=dst, in_=src), reads=[wunit_r[name][u]], writes=[WRr[i]], dma=f"w{i}")
            return WR[:, i, :], WRr[i]

        cast_list = []
        for name in ["w_xr", "w_g", "w_kd", "w_v", "w_yr", "w_q", "w_gr", "w_rnn", "w_ga", "w_attn", "w_out", "w_up", "w_down"]:
            nu = dict((n, k) for n, k, _ in WSPEC)[name]
            cast_list.extend((name, u) for u in range(nu))

        def cast_some(n):
            while n > 0 and st["cast"] < len(cast_list):
                k = st["cast"]
                st["cast"] += 1
                n -= 1
                name, u = cast_list[k]
                src = wext[name][u]
                dst = wbf[name][u]
                S.op("pool", lambda e, src=src, dst=dst: e.dma_start(out=dst, in_=src),
                     writes=[wunit_r[name][u], cs_r[k % NCS]], dma=f"c{k % NCS}")

        def proj(name, u, rhs_of_kc, rhs_res_of_kc, nk=KC, koff=0, pb=None, first=True, last=True):
            W, Wr_ = wload(name, u)
            if pb is None:
                pb = bank()
            pa, pr = pb
            for kc in range(nk):
                lhsT = W[:, kc * 128:(kc + 1) * 128]
                rhs = rhs_of_kc(koff + kc)
                S.op("pe", lambda e, lhsT=lhsT, rhs=rhs, s=(first and kc == 0), t=(last and kc == nk - 1):
                     e.matmul(pa[:], lhsT=lhsT, rhs=rhs, start=s, stop=t),
                     reads=[Wr_] + rhs_res_of_kc(koff + kc), writes=[pr])
            return pb

        H_rhs = lambda kc: H[:, kc, :]
        H_res = lambda kc: [Hr[kc]]

        S.op("sp", lambda e: e.dma_start(out=PPt[:], in_=pp_d[:, :]), writes=[r_pp], dma="pp")
        S.op("sp", lambda e: e.dma_start(out=CST[:], in_=cst_d[:, :]), writes=[r_cst], dma="cst")
        cast_some(32)
        S.op("dve", lambda e: e.tensor_copy(out=permb[:], in_=CST[:, 0:128]), reads=[r_cst], writes=[r_perm])
        S.op("dve", lambda e: e.tensor_copy(out=mprev[:], in_=CST[:, 128:256]), reads=[r_cst], writes=[r_mprev])
        S.op("dve", lambda e: e.tensor_copy(out=mown[:], in_=CST[:, 256:384]), reads=[r_cst], writes=[r_mown])
        S.op("dve", lambda e: e.tensor_scalar(out=mprev0[:], in0=CST[:, 128:256], scalar1=PPt[:, PC_FLAG:PC_FLAG + 1], scalar2=None, op0=ALU.mult),
             reads=[r_cst, r_pp], writes=[r_mprev0])
        S.op("dve", lambda e: e.memset(onesb[:], 1.0), writes=[r_ones])
        S.op("dve", lambda e: e.memset(hstate[:], 0.0), writes=hst_r)
        S.op("dve", lambda e: e.memset(xhalo[:], 0.0), writes=xh_r)
        S.op("dve", lambda e: e.memset(Kr[:], 0.0), writes=Kr_r + Krh_r)
        S.op("dve", lambda e: e.memset(Vt[:], 0.0), writes=Vt_r)
        lam = PPt[:, PC_LAM:PC_LAM + 16]
        e_, w_, lnw, d_, rd, sp_ = [sm[:, i, :] for i in range(6)]
        S.op("act", lambda e: e.activation(out=e_, in_=lam, func=AF.Exp, scale=-1.0), reads=[r_pp], writes=[r_sm])
        S.op("dve", lambda e: e.tensor_scalar(out=w_, in0=e_, scalar1=1.0, scalar2=None, op0=ALU.add), reads=[r_sm], writes=[r_sm])
        S.op("act", lambda e: e.activation(out=lnw, in_=w_, func=AF.Ln), reads=[r_sm], writes=[r_sm])
        S.op("dve", lambda e: e.tensor_scalar(out=d_, in0=w_, scalar1=1.0, scalar2=1e-30, op0=ALU.subtract, op1=ALU.max), reads=[r_sm], writes=[r_sm])
        S.op("dve", lambda e: e.reciprocal(out=rd, in_=d_), reads=[r_sm], writes=[r_sm])
        S.op("dve", lambda e: e.tensor_tensor(out=sp_, in0=lnw, in1=e_, op=ALU.mult), reads=[r_sm], writes=[r_sm])
        S.op("dve", lambda e: e.tensor_tensor(out=sp_, in0=sp_, in1=rd, op=ALU.mult), reads=[r_sm], writes=[r_sm])
        S.op("dve", lambda e: e.tensor_scalar(out=nsp8[:], in0=sp_, scalar1=-8.0, scalar2=None, op0=ALU.mult), reads=[r_sm], writes=[r_nsp])
        S.op("dve", lambda e: e.tensor_scalar(out=nsp16[:], in0=sp_, scalar1=-16.0, scalar2=None, op0=ALU.mult), reads=[r_sm], writes=[r_nsp])
        S.op("act", lambda e: e.activation(out=esink[:], in_=PPt[:, PC_SINK:PC_SINK + 32], func=AF.Exp), reads=[r_pp], writes=[r_esink])

        def load_x_group(src, t0, q):
            S.op("sp", lambda e: e.dma_start(out=X[:, 4 * q:4 * q + 4, :],
                                             in_=src[4 * q * 128:(4 * q + 4) * 128, t0:t0 + T].rearrange("(c p) t -> p c t", p=128)),
                 writes=Xr[4 * q:4 * q + 4], dma=f"x{q}")

        def load_x(src, t0):
            for q in range(4):
                load_x_group(src, t0, q)

        def load_pos(src, t0):
            S.op("sp", lambda e: e.dma_start(out=posi[:], in_=src[:, t0:t0 + T]), writes=[r_posi], dma="pos")
            S.op("dve", lambda e: e.tensor_copy(out=posf[:], in_=posi[:]), reads=[r_posi], writes=[r_posf])
            S.op("dve", lambda e: e.tensor_scalar(out=resetb[:], in0=posf[:], scalar1=0.0, scalar2=BIGR, op0=ALU.is_equal, op1=ALU.mult),
                 reads=[r_posf], writes=[r_resetb])

        def rope_tables():
            invf = PPt[:, PC_INVF:PC_INVF + 1]
            for dst, dst_r, off in ((sinT, r_sin, 0.0), (cosT, r_cos, 0.25)):
                ta, tar = osc()
                tk, tkr = osc()
                S.op("dve", lambda e, ta=ta, off=off: e.tensor_scalar(out=ta, in0=posf[:], scalar1=invf, scalar2=off, op0=ALU.mult, op1=ALU.add),
                     reads=[r_posf, r_pp], writes=[tar])
                S.op("dve", lambda e, ta=ta, tk=tk: e.tensor_scalar(out=tk, in0=ta, scalar1=MAGIC, scalar2=MAGIC, op0=ALU.add, op1=ALU.subtract),
                     reads=[tar], writes=[tkr])
                S.op("dve", lambda e, ta=ta, tk=tk: e.tensor_tensor(out=ta, in0=ta, in1=tk, op=ALU.subtract), reads=[tar, tkr], writes=[tar])
                S.op("act", lambda e, ta=ta, dst=dst: e.activation(out=dst[:], in_=ta, func=AF.Sin, scale=TWO_PI_S), reads=[tar], writes=[dst_r])

        def finish_rstd():
            S.op("act", lambda e: e.activation(out=rstd[:], in_=SSQ[:], func=AF.Sqrt, scale=1.0 / D, bias=EPS), reads=[SSQr], writes=[r_rstd])
            S.op("dve", lambda e: e.reciprocal(out=rstd[:], in_=rstd[:]), reads=[r_rstd], writes=[r_rstd])

        def ssq_mm(i, c):
            S.op("pe", lambda e: e.matmul(SSQ[:], lhsT=onesb[:], rhs=sq[:, i, :], start=(c == 0), stop=(c == KC - 1)),
                 reads=[sq_r[i], r_ones], writes=[SSQr])

        def norm_X_to_H(gcol):
            for c in range(KC):
                i = c % 2
                S.op("act", lambda e, c=c, i=i: e.activation(out=sq[:, i, :], in_=X[:, c, :], func=AF.Square), reads=[Xr[c]], writes=[sq_r[i]])
                ssq_mm(i, c)
            finish_rstd()
            for c in range(KC):
                S.op("dve", lambda e, c=c: e.scalar_tensor_tensor(out=H[:, c, :], in0=X[:, c, :], scalar=PPt[:, gcol + c:gcol + c + 1], in1=rstd[:],
                                                                 op0=ALU.mult, op1=ALU.mult), reads=[Xr[c], r_pp, r_rstd], writes=[Hr[c]])

        def rnn_phase(with_y, cast_per=0):
            Bk = [dict() for _ in range(8)]

            def stA(nb):
                b = Bk[nb]
                par = nb % 2
                b["xc"] = []
                for j in range(2):
                    c = 2 * nb + j
                    pa, pr = proj("w_xr", c, H_rhs, H_res)
                    S.op("dve", lambda e, c=c, j=j: e.tensor_copy(out=XP[:, j, 0:3], in_=xhalo[:, c, 0:3]), reads=[xh_r[c]], writes=[XP_r[j]])
                    S.op("act", lambda e, j=j, pa=pa: e.activation(out=XP[:, j, 3:515], in_=pa[:], func=AF.Copy), reads=[pr], writes=[XP_r[j]])
                    S.op("dve", lambda e, c=c, j=j: e.tensor_copy(out=xhalo[:, c, 0:3], in_=XP[:, j, 512:515]), reads=[XP_r[j]], writes=[xh_r[c]])
                    xc, xcr = rsc()
                    cw = PC_CONVW + 4 * c
                    S.op("dve", lambda e, j=j, xc=xc, cw=cw, c=c: e.tensor_scalar(out=xc, in0=XP[:, j, 3:515], scalar1=PPt[:, cw + 3:cw + 4],
                                                                                 scalar2=PPt[:, PC_CONVB + c:PC_CONVB + c + 1], op0=ALU.mult, op1=ALU.add),
                         reads=[XP_r[j], r_pp], writes=[xcr])
                    for k in range(3):
                        S.op("dve", lambda e, j=j, xc=xc, cw=cw, k=k: e.scalar_tensor_tensor(out=xc, in0=XP[:, j, k:k + 512], scalar=PPt[:, cw + k:cw + k + 1],
                                                                                            in1=xc, op0=ALU.mult, op1=ALU.add),
                             reads=[XP_r[j], r_pp, xcr], writes=[xcr])
                    xi = 2 * par + j
                    S.op("pool", lambda e, xi=xi, xc=xc: e.tensor_copy(out=xcb[:, xi, :], in_=xc), reads=[xcr], writes=[xcb_r[xi]])
                    b["xc"].append((xc, xcr))

            def stB1a(nb):
                b = Bk[nb]
                par = nb % 2
                Wg, Wgr = wload("w_g", nb)
                grp = []
                for j in range(2):
                    for gi in range(2):
                        pa, pr = bank()
                        for kc in range(2):
                            o0 = gi * 512 + kc * 256 + j * 128
                            xi = 2 * par + kc
                            S.op("pe", lambda e, pa=pa, o0=o0, kc=kc, xi=xi, Wg=Wg: e.matmul(pa[:], lhsT=Wg[:, o0:o0 + 128], rhs=xcb[:, xi, :], start=(kc == 0), stop=(kc == 1)),
                                 reads=[Wgr, xcb_r[xi]], writes=[pr])
                        grp.append((j, gi, pa, pr))
                for j, gi, pa, pr in grp:
                    c = 2 * nb + j
                    bcol = PC_BA if gi == 0 else PC_BX
                    gt, gtr = rsc()
                    S.op("act", lambda e, pa=pa, gt=gt, bcol=bcol, c=c: e.activation(out=gt, in_=pa[:], func=AF.Sigmoid, bias=PPt[:, bcol + c:bcol + c + 1]),
                         reads=[pr, r_pp], writes=[gtr])
                    b[(j, gi)] = (gt, gtr)
                for j in range(2):
                    rt, rr = b[(j, 0)]
                    S.op("pool", lambda e, rt=rt: e.tensor_tensor(out=rt, in0=rt, in1=resetb[:], op=ALU.add), reads=[rr, r_resetb], writes=[rr])

            def stB1b(nb):
                b = Bk[nb]
                for j in range(2):
                    c = 2 * nb + j
                    rt, rr = b[(j, 0)]
                    at, ar = rsc()
                    mt, mr = rsc()
                    S.op("act", lambda e, rt=rt, at=at, c=c: e.activation(out=at, in_=rt, func=AF.Exp, scale=nsp8[:, c:c + 1]), reads=[rr, r_nsp], writes=[ar])
                    S.op("act", lambda e, rt=rt, mt=mt, c=c: e.activation(out=mt, in_=rt, func=AF.Exp, scale=nsp16[:, c:c + 1]), reads=[rr, r_nsp], writes=[mr])
                    b[("a", j)] = (at, ar)
                    b[("m", j)] = (mt, mr)
                for j in range(2):
                    mt, mr = b[("m", j)]
                    S.op("act", lambda e, mt=mt: e.activation(out=mt, in_=mt, func=AF.Sqrt, scale=-1.0, bias=1.0), reads=[mr], writes=[mr])
                for j in range(2):
                    it, ir = b[(j, 1)]
                    xc, xcr = b["xc"][j]
                    S.op("pool", lambda e, it=it, xc=xc: e.tensor_tensor(out=it, in0=it, in1=xc, op=ALU.mult), reads=[ir, xcr], writes=[ir])
                for j in range(2):
                    it, ir = b[(j, 1)]
                    mt, mr = b[("m", j)]
                    S.op("pool", lambda e, it=it, mt=mt: e.tensor_tensor(out=it, in0=it, in1=mt, op=ALU.mult), reads=[ir, mr], writes=[ir])

            def stB2(nb):
                b = Bk[nb]
                yp = []
                if with_y:
                    for j in range(2):
                        yp.append(proj("w_yr", 2 * nb + j, H_rhs, H_res))
                hts = []
                for j in range(2):
                    c = 2 * nb + j
                    at, ar = b[("a", j)]
                    it, ir = b[(j, 1)]
                    ht, hr = rsc()
                    S.op("dve", lambda e, at=at, it=it, ht=ht, c=c: e.tensor_tensor_scan(out=ht, data0=at, data1=it, initial=hstate[:, c:c + 1], op0=ALU.mult, op1=ALU.add),
                         reads=[ar, ir, hst_r[c]], writes=[hr])
                    S.op("dve", lambda e, ht=ht, c=c: e.tensor_copy(out=hstate[:, c:c + 1], in_=ht[:, T - 1:T]), reads=[hr], writes=[hst_r[c]])
                    hts.append((ht, hr))
                if with_y:
                    gys = []
                    for j in range(2):
                        pa, pr = yp[j]
                        gy, gyr = rsc()
                        S.op("act", lambda e, pa=pa, gy=gy: e.activation(out=gy, in_=pa[:], func=AF.Gelu_apprx_tanh), reads=[pr], writes=[gyr])
                        gys.append((gy, gyr))
                    for j in range(2):
                        c = 2 * nb + j
                        ht, hr = hts[j]
                        gy, gyr = gys[j]
                        S.op("pool", lambda e, ht=ht, gy=gy, c=c: e.tensor_tensor(out=R[:, c, :], in0=ht, in1=gy, op=ALU.mult), reads=[hr, gyr], writes=Rr[c])
                Bk[nb] = None
                cast_some(cast_per)

            if not with_y:
                for k in range(-3, 8):
                    if 0 <= k + 3 < 8:
                        stA(k + 3)
                    if 0 <= k + 2 < 8:
                        stB1a(k + 2)
                    if 0 <= k + 1 < 8:
                        stB1b(k + 1)
                    if 0 <= k < 8:
                        stB2(k)
            else:
                for k in range(-2, 8):
                    if 0 <= k + 2 < 8:
                        stA(k + 2)
                    if 0 <= k + 1 < 8:
                        stB1a(k + 1)
                    if 0 <= k < 8:
                        stB2(k)
                    if 0 <= k + 1 < 8:
                        stB1b(k + 1)

        def rope_a(pb):
            pa, pr = pb
            qb, qbr = qtile()
            S.op("act", lambda e: e.activation(out=qb, in_=pa[:], func=AF.Copy), reads=[pr], writes=[qbr])
            return pa, pr, qb, qbr

        def rope_b(stt, dst, dst_res):
            pa, pr, qb, qbr = stt
            pr2a, pr2r = bank()
            S.op("pe", lambda e: e.matmul(pr2a[:], lhsT=permb[:], rhs=qb, start=True, stop=True), reads=[qbr, r_perm], writes=[pr2r])
            t1, t1r = osc()
            t2, t2r = osc()
            S.op("dve", lambda e: e.tensor_tensor(out=t1, in0=pa[:], in1=cosT[:], op=ALU.mult), reads=[pr, r_cos], writes=[t1r])
            S.op("dve", lambda e: e.tensor_tensor(out=t2, in0=pr2a[:], in1=sinT[:], op=ALU.mult), reads=[pr2r, r_sin], writes=[t2r])
            S.op("pool", lambda e: e.tensor_tensor(out=dst, in0=t1, in1=t2, op=ALU.add), reads=[t1r, t2r], writes=dst_res)

        def produce_kv(with_q):
            for g in range(4):
                S.op("pool", lambda e, g=g: e.tensor_copy(out=Kr[:, g, 0:128], in_=Kr[:, g, 512:640]), reads=[Kr_r[g]], writes=[Krh_r[g]])
            S.op("pool", lambda e: e.tensor_copy(out=Vt[:, 0, :], in_=Vt[:, 4, :]), reads=[Vt_r[4]], writes=[Vt_r[0]])
            pbs = [bank() for _ in range(4)]
            for vq in range(4):
                W, Wr_ = wload("w_v", vq)
                for blk in range(4):
                    pa, pr = pbs[blk]
                    for kcl in range(4):
                        kc = vq * 4 + kcl
                        S.op("pe", lambda e, pa=pa, kc=kc, kcl=kcl, blk=blk, W=W: e.matmul(pa[:], lhsT=H[:, kc, blk * 128:(blk + 1) * 128], rhs=W[:, kcl * 512:(kcl + 1) * 512],
                                                                                         start=(kc == 0), stop=(kc == KC - 1)),
                             reads=[Wr_, Hr[kc]], writes=[pr])
            for blk in range(4):
                pa, pr = pbs[blk]
                S.op("act", lambda e, pa=pa, blk=blk: e.activation(out=Vt[:, blk + 1, :], in_=pa[:], func=AF.Copy), reads=[pr], writes=[Vt_r[blk + 1]])
            items = [("w_kd", g, Kr[:, g, 128:640], [Kr_r[g]]) for g in range(4)]
            if with_q:
                items += [("w_q", c, R[:, 16 + c, :], Rr[16 + c]) for c in range(KC)]
            prev = None
            for name, u, dst, dres in items:
                pb = proj(name, u, H_rhs, H_res)
                cur = (rope_a(pb), dst, dres)
                if prev is not None:
                    rope_b(*prev)
                prev = cur
            rope_b(*prev)

        def attention(first_sb):
            st["nbank"] = 8

            def stS(qb, g):
                qs = slice(qb * 128, (qb + 1) * 128)
                pts = {}
                for ch in range(2):
                    kcols = slice(qb * 128 + ch * 128, qb * 128 + ch * 128 + 128)
                    kres = [Krh_r[g]] if (ch == 0 and qb == 0) else [Kr_r[g]]
                    for half in range(2):
                        rows = slice(half * 64, half * 64 + 64)
                        pa, pr = bank()
                        qres = [Rr[16 + 4 * g + jj][qb] for jj in range(4)]
                        S.op("pe", lambda e, pa=pa, g=g, rows=rows, kcols=kcols, qs=qs: e.matmul(pa[:], lhsT=Kr[rows, g, kcols], rhs=R[rows, 16 + 4 * g:16 + 4 * g + 4, qs],
                                                                                             start=True, stop=True),
                             reads=kres + qres, writes=[pr])
                        pts[(ch, half)] = (pa, pr)
                for ch in range(2):
                    for half in range(2):
                        pa, pr = pts[(ch, half)]
                        pt, ptr = ptile()
                        S.op("act", lambda e, pa=pa, pt=pt: e.activation(out=pt, in_=pa[:], func=AF.Exp, scale=0.125), reads=[pr], writes=[ptr])
                        if ch == 0:
                            mk, mkr = (mprev0, r_mprev0) if (first_sb and qb == 0) else (mprev, r_mprev)
                        else:
                            mk, mkr = mown, r_mown
                        S.op("pool" if half == 0 else "dve", lambda e, pt=pt, mk=mk: e.tensor_tensor(out=pt.rearrange("p (h q) -> p h q", h=4), in0=pt.rearrange("p (h q) -> p h q", h=4),
                                                                            in1=mk[:].unsqueeze(1).to_broadcast([128, 4, 128]), op=ALU.mult),
                             reads=[ptr, mkr], writes=[ptr])
                        pts[(ch, half)] = (pt, ptr)
                return pts

            def stP(qb, g, pts):
                qs = slice(qb * 128, (qb + 1) * 128)
                pn, pnr = bank()
                pd, pdr = bank()
                for half in range(2):
                    rows = slice(half * 64, half * 64 + 64)
                    for ch in range(2):
                        pt, ptr = pts[(ch, half)]
                        blk = qb + ch
                        v0 = g * 128 + half * 64
                        S.op("pe", lambda e, rows=rows, blk=blk, v0=v0, pt=pt, ch=ch: e.matmul(pn[rows, :], lhsT=Vt[:, blk, v0:v0 + 64], rhs=pt, start=(ch == 0), stop=(ch == 1)),
                             reads=[Vt_r[blk], ptr], writes=[pnr])
                for half in range(2):
                    rows = slice(half * 64, half * 64 + 64)
                    for ch in range(2):
                        pt, ptr = pts[(ch, half)]
                        S.op("pe", lambda e, rows=rows, pt=pt, ch=ch, half=half: e.matmul(pd[rows, :], lhsT=onesb[:, half * 64:half * 64 + 64], rhs=pt, start=(ch == 0), stop=(ch == 1)),
                             reads=[r_ones, ptr], writes=[pdr])
                rc, rcr = osc()
                sc0 = g * 4
                if (qb * 4 + g) % 2 == 0:
                    for jj in range(4):
                        S.op("act", lambda e, rc=rc, jj=jj, sc0=sc0: e.activation(out=rc[:, jj * 128:(jj + 1) * 128], in_=pd[:, jj * 128:(jj + 1) * 128],
                                                                              func=AF.Ln, bias=esink[:, sc0 + jj:sc0 + jj + 1]),
                             reads=[pdr, r_esink], writes=[rcr])
                else:
                    S.op("dve", lambda e, rc=rc, sc0=sc0: e.tensor_tensor(out=rc.rearrange("p (h q) -> p h q", h=4),
                                                                        in0=pd[:].rearrange("p (h q) -> p h q", h=4),
                                                                        in1=esink[:, sc0:sc0 + 4].unsqueeze(2).to_broadcast([128, 4, 128]), op=ALU.add),
                         reads=[pdr, r_esink], writes=[rcr])
                    S.op("act", lambda e, rc=rc: e.activation(out=rc, in_=rc, func=AF.Ln), reads=[rcr], writes=[rcr])
                S.op("act", lambda e, rc=rc: e.activation(out=rc, in_=rc, func=AF.Exp, scale=-1.0), reads=[rcr], writes=[rcr])
                S.op("dve", lambda e, rc=rc, g=g, qs=qs: e.tensor_tensor(out=R[:, 16 + 4 * g:16 + 4 * g + 4, qs],
                                                                     in0=pn[:].rearrange("p (h q) -> p h q", h=4),
                                                                     in1=rc.rearrange("p (h q) -> p h q", h=4), op=ALU.mult),
                     reads=[pnr, rcr], writes=[Rr[16 + 4 * g + jj][qb] for jj in range(4)])

            prev = None
            for qb in range(4):
                for g in range(4):
                    pts = stS(qb, g)
                    if prev is not None:
                        stP(*prev)
                    prev = (qb, g, pts)
            stP(*prev)
            st["nbank"] = 7

        def mix_phase():
            YR_rhs = lambda kc: R[:, kc, :]
            YR_res = lambda kc: Rr[kc]
            YA_rhs = lambda kc: R[:, 16 + kc, :]
            YA_res = lambda kc: Rr[16 + kc]
            for n in range(KC):
                pg, pgr = proj("w_gr", n, H_rhs, H_res)
                s1, s1r = osc()
                S.op("act", lambda e, pg=pg, s1=s1: e.activation(out=s1, in_=pg[:], func=AF.Sigmoid), reads=[pgr], writes=[s1r])
                p1, p1r = proj("w_rnn", n, YR_rhs, YR_res)
                S.op("dve", lambda e, p1=p1, s1=s1: e.tensor_tensor(out=s1, in0=p1[:], in1=s1, op=ALU.mult), reads=[p1r, s1r], writes=[s1r])
                pg2, pg2r = proj("w_ga", n, H_rhs, H_res)
                s2, s2r = osc()
                S.op("act", lambda e, pg2=pg2, s2=s2: e.activation(out=s2, in_=pg2[:], func=AF.Sigmoid), reads=[pg2r], writes=[s2r])
                p2, p2r = proj("w_attn", n, YA_rhs, YA_res)
                S.op("dve", lambda e, p2=p2, s2=s2: e.tensor_tensor(out=s2, in0=p2[:], in1=s2, op=ALU.mult), reads=[p2r, s2r], writes=[s2r])
                S.op("pool", lambda e, s1=s1, s2=s2, n=n: e.tensor_tensor(out=R[:, 32 + n, :], in0=s1, in1=s2, op=ALU.add), reads=[s1r, s2r], writes=Rr[32 + n])

        def proj_to_O_with_ssq(groups):
            pend = None
            for n in range(KC):
                pa, pr = groups(n)
                if pend is not None:
                    ssq_mm(*pend)
                i = n % 2
                S.op("act", lambda e, pa=pa, n=n: e.activation(out=O[:, n, :], in_=pa[:], func=AF.Copy), reads=[pr], writes=[Or[n]])
                S.op("act", lambda e, pa=pa, i=i: e.activation(out=sq[:, i, :], in_=pa[:], func=AF.Square), reads=[pr], writes=[sq_r[i]])
                pend = (i, n)
            ssq_mm(*pend)
            finish_rstd()

        def out_phase():
            MX_rhs = lambda kc: R[:, 32 + kc, :]
            MX_res = lambda kc: Rr[32 + kc]
            proj_to_O_with_ssq(lambda n: proj("w_out", n, MX_rhs, MX_res))
            for n in range(KC):
                S.op("dve", lambda e, n=n: e.scalar_tensor_tensor(out=O[:, n, :], in0=O[:, n, :], scalar=PPt[:, PC_GPOST + n:PC_GPOST + n + 1], in1=rstd[:],
                                                                 op0=ALU.mult, op1=ALU.mult), reads=[Or[n], r_pp, r_rstd], writes=[Or[n]])
                S.op("dve" if n % 4 == 3 else "pool", lambda e, n=n: e.tensor_tensor(out=X[:, n, :], in0=O[:, n, :], in1=X[:, n, :], op=ALU.add), reads=[Or[n], Xr[n]], writes=[Xr[n]])

        def mlp_phase(t0, next_load=None):
            norm_X_to_H(PC_GMPRE)
            for j in range(64):
                pa, pr = proj("w_up", j, H_rhs, H_res)
                tt, ttr = osc()
                S.op("act", lambda e, pa=pa, tt=tt: e.activation(out=tt, in_=pa[:], func=AF.Relu), reads=[pr], writes=[ttr])
                S.op("pool", lambda e, tt=tt, j=j: e.tensor_tensor(out=R[:, j, :], in0=tt, in1=tt, op=ALU.mult), reads=[ttr], writes=Rr[j])

            def down_group(n):
                pb = bank()
                for kq in range(4):
                    proj("w_down", n * 4 + kq, lambda kc: R[:, kc, :], lambda kc: Rr[kc], koff=kq * 16, pb=pb, first=(kq == 0), last=(kq == 3))
                return pb

            proj_to_O_with_ssq(down_group)
            outs = []
            for n in range(KC):
                S.op("dve", lambda e, n=n: e.scalar_tensor_tensor(out=O[:, n, :], in0=O[:, n, :], scalar=PPt[:, PC_GMPOST + n:PC_GMPOST + n + 1], in1=rstd[:],
                                                                 op0=ALU.mult, op1=ALU.mult), reads=[Or[n], r_pp, r_rstd], writes=[Or[n]])
                S.op("dve" if n % 4 == 3 else "pool", lambda e, n=n: e.tensor_tensor(out=O[:, n, :], in0=O[:, n, :], in1=X[:, n, :], op=ALU.add), reads=[Or[n], Xr[n]], writes=[Or[n]])
                outs.append(S.op("sp", lambda e, n=n: e.dma_start(out=yT[n * 128:(n + 1) * 128, t0:t0 + T], in_=O[:, n, :]), reads=[Or[n]], dma=f"o{n}"))
                if next_load is not None and n % 4 == 3:
                    next_load(n // 4)
            return outs

        import os
        STAGE = int(os.environ.get("KSTAGE", "99"))
        for pbi in range(NPB if STAGE >= 1 else 0):
            t0 = pbi * T
            load_x(xpT, t0)
            load_pos(pospb, t0)
            norm_X_to_H(PC_GPRE)
            rnn_phase(False, cast_per=-(-(len(cast_list) - 32) // (NPB * 8)))
            if pbi == NPB - 1 and STAGE >= 2:
                rope_tables()
                produce_kv(False)
        S.op("dve", lambda e: e.tensor_scalar(out=hstate[:], in0=hstate[:], scalar1=PPt[:, PC_FLAG:PC_FLAG + 1], scalar2=None, op0=ALU.mult),
             reads=hst_r + [r_pp], writes=hst_r)
        S.op("dve", lambda e: e.tensor_scalar(out=xhalo[:], in0=xhalo[:], scalar1=PPt[:, PC_FLAG:PC_FLAG + 1], scalar2=None, op0=ALU.mult),
             reads=xh_r + [r_pp], writes=xh_r)

        all_outs = []
        for sbi in range(NSB if STAGE >= 3 else 0):
            t0 = sbi * T
            if sbi == 0 or STAGE < 8:
                load_x(xT, t0)
            load_pos(posb, t0)
            norm_X_to_H(PC_GPRE)
            rnn_phase(True, cast_per=(-(-(len(cast_list) - st["cast"]) // 8) if sbi == 0 else 0))
            cast_some(10 ** 9)
            if STAGE < 4:
                continue
            rope_tables()
            produce_kv(True)
            if STAGE < 5:
                continue
            attention(first_sb=(sbi == 0))
            if STAGE < 6:
                continue
            mix_phase()
            if STAGE < 7:
                continue
            out_phase()
            if STAGE < 8:
                continue
            nl = (lambda q, t1=t0 + T: load_x_group(xT, t1, q)) if sbi + 1 < NSB else None
            all_outs.extend(mlp_phase(t0, nl))
        S.fence("sp", all_outs)
        S.emit(es)
    return nc


def _units(W):
    K, N = W.shape
    a = W.reshape(16, 128, N // 128, 128).transpose(2, 1, 0, 3)
    return np.ascontiguousarray(a).reshape(N // 128, 128, 2048)


def _prep_weights(inp):
    w_in = np.asarray(inp["w_in"][0])
    o = np.cumsum([0, 2048, 2048, 2048, 256, 256, 2048, 2048])
    xr, yr, q, k, v, gr, ga = [w_in[:, o[i]:o[i + 1]] for i in range(7)]
    kd = np.concatenate([np.concatenate([k[:, 64 * g:64 * g + 64]] * 2, axis=1) for g in range(4)], axis=1)
    vd = np.concatenate([np.concatenate([v[:, 64 * g:64 * g + 64]] * 2, axis=1) for g in range(4)], axis=1)
    vu = np.ascontiguousarray(vd.reshape(4, 4, 128, 512).transpose(0, 2, 1, 3)).reshape(4, 128, 2048)
    wa = np.asarray(inp["w_rg_a"][0]).reshape(8, 2, 128, 256).transpose(0, 2, 1, 3).reshape(8, 128, 512)
    wx = np.asarray(inp["w_rg_x"][0]).reshape(8, 2, 128, 256).transpose(0, 2, 1, 3).reshape(8, 128, 512)
    wg = np.ascontiguousarray(np.concatenate([wa, wx], axis=2))
    wd = np.asarray(inp["w_mlp_down"][0]).reshape(4, 16, 128, 16, 128).transpose(3, 0, 2, 1, 4)
    wd = np.ascontiguousarray(wd).reshape(64, 128, 2048)
    return {
        "w_xr": _units(xr), "w_yr": _units(yr), "w_q": _units(q), "w_kd": _units(kd), "w_v": vu,
        "w_gr": _units(gr), "w_ga": _units(ga), "w_g": wg,
        "w_rnn": _units(np.asarray(inp["w_rnn_proj"][0])), "w_attn": _units(np.asarray(inp["w_attn_proj"][0])),
        "w_out": _units(np.asarray(inp["w_out"][0])), "w_up": _units(np.asarray(inp["w_mlp_up"][0])), "w_down": wd,
    }


def _consts():
    cst = np.zeros((128, 384), np.float32)
    for m in range(128):
        d = m % 64
        if d < 32:
            cst[m + 32, m] = -1.0
        else:
            cst[m - 32, m] = 1.0
    kk = np.arange(128)[:, None]
    qq = np.arange(128)[None, :]
    cst[:, 128:256] = (kk > qq)
    cst[:, 256:384] = (kk <= qq)
    return cst


def _params(inp, flag):
    pp = np.zeros((128, NPP), np.float32)
    col = lambda v: np.asarray(v).reshape(16, 128).T
    pp[:, PC_GPRE:PC_GPRE + 16] = col(inp["norm_mix_pre"][0])
    pp[:, PC_GPOST:PC_GPOST + 16] = col(inp["norm_mix_post"][0])
    pp[:, PC_GMPRE:PC_GMPRE + 16] = col(inp["norm_mlp_pre"][0])
    pp[:, PC_GMPOST:PC_GMPOST + 16] = col(inp["norm_mlp_post"][0])
    cw = np.asarray(inp["conv_w"][0])
    pp[:, PC_CONVW:PC_CONVW + 64] = cw.reshape(4, 16, 128).transpose(2, 1, 0).reshape(128, 64)
    pp[:, PC_CONVB:PC_CONVB + 16] = col(inp["conv_b"][0])
    pp[:, PC_BA:PC_BA + 16] = col(inp["b_rg_a"][0])
    pp[:, PC_BX:PC_BX + 16] = col(inp["b_rg_x"][0])
    pp[:, PC_LAM:PC_LAM + 16] = col(inp["lru_lambda"][0])
    sk = np.asarray(inp["attn_sinks"][0]).reshape(4, 4, 2)
    pp[0:64, PC_SINK:PC_SINK + 16] = sk[:, :, 0].reshape(16)[None, :]
    pp[64:128, PC_SINK:PC_SINK + 16] = sk[:, :, 1].reshape(16)[None, :]
    invf = (10000.0 ** (-np.arange(0, 64, 2, dtype=np.float64) / 64.0)) / (2.0 * np.pi)
    pp[:, PC_INVF] = invf[(np.arange(128) % 64) % 32].astype(np.float32)
    pp[:, PC_FLAG] = flag
    return pp


_CACHE = {}


def kernel(**inputs):
    x = np.asarray(inputs["x"])
    positions = np.asarray(inputs["positions"])
    B, S_, _ = x.shape
    n_cores = 8
    per = n_cores // B
    TOK = S_ // per
    NSB = TOK // T
    key = (NSB,)
    if key not in _CACHE:
        _CACHE[key] = build_program(NSB, NSB)
    nc = _CACHE[key]
    wts = _prep_weights(inputs)
    cst = _consts()
    in_maps = []
    import os
    for c in range(int(os.environ.get("KCORES", n_cores))):
        b, half = c // per, c % per
        s0 = half * TOK
        m = {
            "xT": np.ascontiguousarray(x[b, s0:s0 + TOK].T),
            "xpT": np.ascontiguousarray(x[b, 0:TOK].T),
            "posb": np.ascontiguousarray(np.broadcast_to(positions[b, s0:s0 + TOK].astype(np.int32)[None], (128, TOK))),
            "pospb": np.ascontiguousarray(np.broadcast_to(positions[b, 0:TOK].astype(np.int32)[None], (128, TOK))),
            "pp": _params(inputs, float(half)),
            "cst": cst,
        }
        m.update(wts)
        in_maps.append(m)
    import os
    ncr = int(os.environ.get("KCORES", n_cores))
    res = run_bass_kernel_spmd(nc, in_maps[:ncr], core_ids=list(range(ncr)))
    out = np.zeros((B, S_, D), np.float32)
    for c in range(ncr):
        b, half = c // per, c % per
        s0 = half * TOK
        out[b, s0:s0 + TOK] = res.results[c]["yT"].T
    return out
```

```python
import contextlib
from contextlib import ExitStack

import numpy as np
import concourse.bass as bass
import concourse.mybir as mybir
from concourse.bass_utils import run_bass_kernel_spmd

F32 = mybir.dt.float32
BF16 = mybir.dt.bfloat16
I32 = mybir.dt.int32
AF = mybir.ActivationFunctionType
ALU = mybir.AluOpType

D = 2048
KC = 16
T = 512
NW = 6
NCS = 32
EPS = 1e-6
BIGR = 1.0e6
MAGIC = 12582912.0
TWO_PI_S = 6.283185307179586 * (1.0 - 2e-6)

ENGS = ("pe", "act", "dve", "pool", "sp")


class Res:
    __slots__ = ("name", "writer", "readers", "excl")

    def __init__(self, name, excl=False):
        self.name = name
        self.writer = None
        self.readers = []
        self.excl = excl


class Ins:
    __slots__ = ("eng", "fn", "deps", "signal", "tok", "dma", "idx")

    def __init__(self, eng, fn, dma):
        self.eng = eng
        self.fn = fn
        self.deps = []
        self.signal = False
        self.tok = None
        self.dma = dma
        self.idx = -1


class Sched:
    def __init__(self, nc):
        self.nc = nc
        self.streams = {e: [] for e in ENGS}
        self.n = 0
        import os
        self.maxops = int(os.environ.get("KMAXOPS", "100000000"))

    def op(self, eng, fn, reads=(), writes=(), dma=None):
        if self.n >= self.maxops:
            return None
        ins = Ins(eng, fn, dma)
        ins.idx = self.n
        self.n += 1
        deps = {}

        def flat(xs):
            out = []
            for x_ in xs:
                if isinstance(x_, (list, tuple)):
                    out.extend(flat(x_))
                else:
                    out.append(x_)
            return out

        reads = flat(reads)
        writes = flat(writes)

        def add(d, raw):
            if d is None:
                return
            if d.dma is None and dma is None and d.eng == eng:
                if eng == "pe" or not raw:
                    return
            deps[id(d)] = d

        for r in reads:
            add(r.writer, True)
            if r.excl:
                for rd in r.readers:
                    if rd.eng != eng:
                        add(rd, False)
        for w in writes:
            add(w.writer, False)
            for rd in w.readers:
                add(rd, False)
        ins.deps = list(deps.values())
        for d in ins.deps:
            d.signal = True
        for r in reads:
            r.readers.append(ins)
        for w in writes:
            w.writer = ins
            w.readers = []
        self.streams[eng].append(ins)
        return ins

    def fence(self, eng, inss):
        ins = Ins(eng, None, None)
        ins.idx = self.n
        self.n += 1
        ins.deps = [i for i in inss if i is not None]
        for d in ins.deps:
            d.signal = True
        self.streams[eng].append(ins)
        return ins

    def emit(self, stack):
        nc = self.nc
        eng_sem = {e: stack.enter_context(nc.semaphore("S_" + e)) for e in ENGS}
        dma_sem = {}
        dma_cnt = {}
        allins = []
        for e in ENGS:
            allins.extend(self.streams[e])
        allins.sort(key=lambda i: i.idx)
        cnt = {e: 0 for e in ENGS}
        for ins in allins:
            if ins.dma is not None:
                if ins.dma not in dma_sem:
                    dma_sem[ins.dma] = stack.enter_context(nc.semaphore("D_" + ins.dma))
                    dma_cnt[ins.dma] = 0
                dma_cnt[ins.dma] += 16
                ins.tok = (dma_sem[ins.dma], dma_cnt[ins.dma])
            elif ins.signal and ins.fn is not None:
                cnt[ins.eng] += 1
                ins.tok = (eng_sem[ins.eng], cnt[ins.eng])
        self.counts = cnt
        block = stack.enter_context(nc.Block())

        def make_body(e):
            stream = self.streams[e]

            def body(eng):
                waited = {}
                for ins in stream:
                    for d in ins.deps:
                        sem, val = d.tok
                        k = id(sem)
                        if waited.get(k, 0) >= val:
                            continue
                        waited[k] = val
                        eng.wait_ge(sem, val)
                    if ins.fn is None:
                        continue
                    inst = ins.fn(eng)
                    if ins.dma is not None:
                        inst.then_inc(ins.tok[0], 16)
                    elif ins.signal:
                        inst.then_inc(ins.tok[0], 1)

            return body

        block.tensor(make_body("pe"))
        block.scalar(make_body("act"))
        block.vector(make_body("dve"))
        block.gpsimd(make_body("pool"))
        block.sync(make_body("sp"))


PC_GPRE, PC_GPOST, PC_GMPRE, PC_GMPOST = 0, 16, 32, 48
PC_CONVW = 64
PC_CONVB, PC_BA, PC_BX, PC_LAM = 128, 144, 160, 176
PC_SINK = 192
PC_INVF = 224
PC_FLAG = 225
NPP = 228

WSPEC = [
    ("w_xr", 16, 2048), ("w_g", 8, 1024), ("w_kd", 4, 2048), ("w_v", 4, 2048),
    ("w_yr", 16, 2048), ("w_q", 16, 2048), ("w_gr", 16, 2048), ("w_rnn", 16, 2048),
    ("w_ga", 16, 2048), ("w_attn", 16, 2048), ("w_out", 16, 2048), ("w_up", 64, 2048),
    ("w_down", 64, 2048),
]


def build_program(NSB, NPB):
    TOK = NSB * T
    TOKP = NPB * T
    nc = bass.Bass("TRN2", target_bir_lowering=False)
    dt_in = lambda name, shape, dt=F32: nc.dram_tensor(name, shape, dt, kind="ExternalInput").ap()
    xT = dt_in("xT", [D, TOK])
    xpT = dt_in("xpT", [D, TOKP])
    posb = dt_in("posb", [128, TOK], I32)
    pospb = dt_in("pospb", [128, TOKP], I32)
    pp_d = dt_in("pp", [128, NPP])
    cst_d = dt_in("cst", [128, 384])
    yT = nc.dram_tensor("yT", [D, TOK], F32, kind="ExternalOutput").ap()
    wext = {}
    wbf = {}
    for name, nu, cols in WSPEC:
        wext[name] = dt_in(name, [nu, 128, cols])
        wbf[name] = nc.dram_tensor(name + "_bf", [nu, 128, cols], BF16, kind="Internal").ap()

    es = ExitStack()
    with es:
        def sb(name, shape, dt):
            return es.enter_context(nc.sbuf_tensor(name, shape, dt))

        X = sb("X", [128, KC, T], F32)
        H = sb("H", [128, KC, T], BF16)
        O = sb("O", [128, KC, T], F32)
        R = sb("R", [128, 64, T], BF16)
        WR = sb("WR", [128, NW, 2048], BF16)
        PPt = sb("PPt", [128, NPP], F32)
        CST = sb("CST", [128, 384], F32)
        permb = sb("permb", [128, 128], BF16)
        mprev = sb("mprev", [128, 128], BF16)
        mprev0 = sb("mprev0", [128, 128], BF16)
        mown = sb("mown", [128, 128], BF16)
        onesb = sb("onesb", [128, 128], BF16)
        nsp8 = sb("nsp8", [128, 16], F32)
        nsp16 = sb("nsp16", [128, 16], F32)
        esink = sb("esink", [128, 32], F32)
        sm = sb("sm", [128, 8, 16], F32)
        hstate = sb("hstate", [128, 16], F32)
        xhalo = sb("xhalo", [128, 16, 4], F32)
        posi = sb("posi", [128, T], I32)
        posf = sb("posf", [128, T], F32)
        resetb = sb("resetb", [128, T], F32)
        cosT = sb("cosT", [128, T], F32)
        sinT = sb("sinT", [128, T], F32)
        sq = sb("sq", [128, 2, T], BF16)
        rstd = sb("rstd", [128, T], F32)
        XP = sb("XP", [128, 2, 516], F32)
        xcb = sb("xcb", [128, 4, T], BF16)
        Kr = sb("Kr", [128, 4, 640], BF16)
        Vt = sb("Vt", [128, 5, 512], BF16)
        banks = [es.enter_context(nc.psum_tensor(f"pb{i}", [128, T], F32)) for i in range(8)]

        S = Sched(nc)
        Xr = [Res(f"X{c}") for c in range(KC)]
        Hr = [Res(f"H{c}") for c in range(KC)]
        Or = [Res(f"O{c}") for c in range(KC)]
        Rr = [[Res(f"R{j}")] for j in range(64)]
        for c in range(16):
            Rr[16 + c] = [Res(f"QA{c}_{q}") for q in range(4)]
        WRr = [Res(f"W{i}") for i in range(NW)]
        Br = [Res(f"B{i}", excl=True) for i in range(8)]
        r_pp, r_cst, r_perm, r_mprev, r_mprev0, r_mown, r_ones = [Res(n) for n in "pp cst perm mprev mprev0 mown ones".split()]
        r_nsp, r_esink, r_sm = Res("nsp"), Res("esink"), Res("sm")
        hst_r = [Res(f"hst{c}") for c in range(16)]
        xh_r = [Res(f"xh{c}") for c in range(16)]
        r_posi, r_posf, r_resetb, r_cos, r_sin, r_rstd = [Res(n) for n in "posi posf resetb cos sin rstd".split()]
        sq_r = [Res("sq0"), Res("sq1")]
        XP_r = [Res("XP0"), Res("XP1")]
        xcb_r = [Res(f"xcb{i}") for i in range(4)]
        Kr_r = [Res(f"Kr{g}") for g in range(4)]
        Krh_r = [Res(f"Krh{g}") for g in range(4)]
        Vt_r = [Res(f"Vt{b}") for b in range(5)]
        wunit_r = {name: [Res(f"{name}{u}") for u in range(nu)] for name, nu, cols in WSPEC}
        cs_r = [Res(f"cs{i}") for i in range(NCS)]

        st = {"bank": 0, "osc": 0, "w": 0, "pt": 0, "cast": 0}

        def bank():
            n = st.get("nbank", 7)
            i = st["bank"] % n
            st["bank"] = (i + 1) % n
            return banks[i], Br[i]

        def rsc():
            i = st.get("rsc", 0)
            st["rsc"] = (i + 1) % 40
            if i < 16:
                return O[:, i, :], [Or[i]]
            j = 16 + 2 * (i - 16)
            return R[:, j:j + 2, :].rearrange("p a t -> p (a t)").bitcast(F32), Rr[j] + Rr[j + 1]

        SSQ, SSQr = banks[7], Br[7]

        def osc():
            i = st["osc"]
            st["osc"] = (i + 1) % KC
            return O[:, i, :], Or[i]

        def ptile():
            i = st["pt"]
            st["pt"] = (i + 1) % 8
            return R[:, 48 + i, :], Rr[48 + i][0]

        def qtile():
            i = st.get("qt", 0)
            st["qt"] = (i + 1) % 4
            return R[:, 56 + i, :], Rr[56 + i][0]

        def wload(name, u):
            cols = dict((n, c) for n, _, c in WSPEC)[name]
            i = st["w"]
            st["w"] = (i + 1) % NW
            src = wbf[name][u]
            dst = WR[:, i, 0:cols]
            S.op("sp", lambda e: e.dma_start(out=dst, in_=src), reads=[wunit_r[name][u]], writes=[WRr[i]], dma=f"w{i}")
            return WR[:, i, :], WRr[i]

        cast_list = []
        for name in ["w_xr", "w_g", "w_kd", "w_v", "w_yr", "w_q", "w_gr", "w_rnn", "w_ga", "w_attn", "w_out", "w_up", "w_down"]:
            nu = dict((n, k) for n, k, _ in WSPEC)[name]
            cast_list.extend((name, u) for u in range(nu))

        def cast_some(n):
            while n > 0 and st["cast"] < len(cast_list):
                k = st["cast"]
                st["cast"] += 1
                n -= 1
                name, u = cast_list[k]
                src = wext[name][u]
                dst = wbf[name][u]
                S.op("pool", lambda e, src=src, dst=dst: e.dma_start(out=dst, in_=src),
                     writes=[wunit_r[name][u], cs_r[k % NCS]], dma=f"c{k % NCS}")

        def proj(name, u, rhs_of_kc, rhs_res_of_kc, nk=KC, koff=0, pb=None, first=True, last=True):
            W, Wr_ = wload(name, u)
            if pb is None:
                pb = bank()
            pa, pr = pb
            for kc in range(nk):
                lhsT = W[:, kc * 128:(kc + 1) * 128]
                rhs = rhs_of_kc(koff + kc)
                S.op("pe", lambda e, lhsT=lhsT, rhs=rhs, s=(first and kc == 0), t=(last and kc == nk - 1):
                     e.matmul(pa[:], lhsT=lhsT, rhs=rhs, start=s, stop=t),
                     reads=[Wr_] + rhs_res_of_kc(koff + kc), writes=[pr])
            return pb

        H_rhs = lambda kc: H[:, kc, :]
        H_res = lambda kc: [Hr[kc]]

        S.op("sp", lambda e: e.dma_start(out=PPt[:], in_=pp_d[:, :]), writes=[r_pp], dma="pp")
        S.op("sp", lambda e: e.dma_start(out=CST[:], in_=cst_d[:, :]), writes=[r_cst], dma="cst")
        cast_some(32)
        S.op("dve", lambda e: e.tensor_copy(out=permb[:], in_=CST[:, 0:128]), reads=[r_cst], writes=[r_perm])
        S.op("dve", lambda e: e.tensor_copy(out=mprev[:], in_=CST[:, 128:256]), reads=[r_cst], writes=[r_mprev])
        S.op("dve", lambda e: e.tensor_copy(out=mown[:], in_=CST[:, 256:384]), reads=[r_cst], writes=[r_mown])
        S.op("dve", lambda e: e.tensor_scalar(out=mprev0[:], in0=CST[:, 128:256], scalar1=PPt[:, PC_FLAG:PC_FLAG + 1], scalar2=None, op0=ALU.mult),
             reads=[r_cst, r_pp], writes=[r_mprev0])
        S.op("dve", lambda e: e.memset(onesb[:], 1.0), writes=[r_ones])
        S.op("dve", lambda e: e.memset(hstate[:], 0.0), writes=hst_r)
        S.op("dve", lambda e: e.memset(xhalo[:], 0.0), writes=xh_r)
        S.op("dve", lambda e: e.memset(Kr[:], 0.0), writes=Kr_r + Krh_r)
        S.op("dve", lambda e: e.memset(Vt[:], 0.0), writes=Vt_r)
        lam = PPt[:, PC_LAM:PC_LAM + 16]
        e_, w_, lnw, d_, rd, sp_ = [sm[:, i, :] for i in range(6)]
        S.op("act", lambda e: e.activation(out=e_, in_=lam, func=AF.Exp, scale=-1.0), reads=[r_pp], writes=[r_sm])
        S.op("dve", lambda e: e.tensor_scalar(out=w_, in0=e_, scalar1=1.0, scalar2=None, op0=ALU.add), reads=[r_sm], writes=[r_sm])
        S.op("act", lambda e: e.activation(out=lnw, in_=w_, func=AF.Ln), reads=[r_sm], writes=[r_sm])
        S.op("dve", lambda e: e.tensor_scalar(out=d_, in0=w_, scalar1=1.0, scalar2=1e-30, op0=ALU.subtract, op1=ALU.max), reads=[r_sm], writes=[r_sm])
        S.op("dve", lambda e: e.reciprocal(out=rd, in_=d_), reads=[r_sm], writes=[r_sm])
        S.op("dve", lambda e: e.tensor_tensor(out=sp_, in0=lnw, in1=e_, op=ALU.mult), reads=[r_sm], writes=[r_sm])
        S.op("dve", lambda e: e.tensor_tensor(out=sp_, in0=sp_, in1=rd, op=ALU.mult), reads=[r_sm], writes=[r_sm])
        S.op("dve", lambda e: e.tensor_scalar(out=nsp8[:], in0=sp_, scalar1=-8.0, scalar2=None, op0=ALU.mult), reads=[r_sm], writes=[r_nsp])
        S.op("dve", lambda e: e.tensor_scalar(out=nsp16[:], in0=sp_, scalar1=-16.0, scalar2=None, op0=ALU.mult), reads=[r_sm], writes=[r_nsp])
        S.op("act", lambda e: e.activation(out=esink[:], in_=PPt[:, PC_SINK:PC_SINK + 32], func=AF.Exp), reads=[r_pp], writes=[r_esink])

        def load_x_group(src, t0, q):
            S.op("sp", lambda e: e.dma_start(out=X[:, 4 * q:4 * q + 4, :],
                                             in_=src[4 * q * 128:(4 * q + 4) * 128, t0:t0 + T].rearrange("(c p) t -> p c t", p=128)),
                 writes=Xr[4 * q:4 * q + 4], dma=f"x{q}")

        def load_x(src, t0):
            for q in range(4):
                load_x_group(src, t0, q)

        def load_pos_dma(src, t0):
            S.op("sp", lambda e: e.dma_start(out=posi[:], in_=src[:, t0:t0 + T]), writes=[r_posi], dma="pos")

        def load_pos(src, t0, dma=True):
            if dma:
                load_pos_dma(src, t0)
            S.op("dve", lambda e: e.tensor_copy(out=posf[:], in_=posi[:]), reads=[r_posi], writes=[r_posf])
            S.op("dve", lambda e: e.tensor_scalar(out=resetb[:], in0=posf[:], scalar1=0.0, scalar2=BIGR, op0=ALU.is_equal, op1=ALU.mult),
                 reads=[r_posf], writes=[r_resetb])

        def rope_tables():
            invf = PPt[:, PC_INVF:PC_INVF + 1]
            for dst, dst_r, off in ((sinT, r_sin, 0.0), (cosT, r_cos, 0.25)):
                ta, tar = osc()
                tk, tkr = osc()
                S.op("dve", lambda e, ta=ta, off=off: e.tensor_scalar(out=ta, in0=posf[:], scalar1=invf, scalar2=off, op0=ALU.mult, op1=ALU.add),
                     reads=[r_posf, r_pp], writes=[tar])
                S.op("dve", lambda e, ta=ta, tk=tk: e.tensor_scalar(out=tk, in0=ta, scalar1=MAGIC, scalar2=MAGIC, op0=ALU.add, op1=ALU.subtract),
                     reads=[tar], writes=[tkr])
                S.op("dve", lambda e, ta=ta, tk=tk: e.tensor_tensor(out=ta, in0=ta, in1=tk, op=ALU.subtract), reads=[tar, tkr], writes=[tar])
                S.op("act", lambda e, ta=ta, dst=dst: e.activation(out=dst[:], in_=ta, func=AF.Sin, scale=TWO_PI_S), reads=[tar], writes=[dst_r])

        def finish_rstd():
            S.op("act", lambda e: e.activation(out=rstd[:], in_=SSQ[:], func=AF.Sqrt, scale=1.0 / D, bias=EPS), reads=[SSQr], writes=[r_rstd])
            S.op("dve", lambda e: e.reciprocal(out=rstd[:], in_=rstd[:]), reads=[r_rstd], writes=[r_rstd])

        def ssq_mm(i, c):
            S.op("pe", lambda e: e.matmul(SSQ[:], lhsT=onesb[:], rhs=sq[:, i, :], start=(c == 0), stop=(c == KC - 1)),
                 reads=[sq_r[i], r_ones], writes=[SSQr])

        def norm_X_to_H(gcol):
            for c in range(KC):
                i = c % 2
                S.op("act", lambda e, c=c, i=i: e.activation(out=sq[:, i, :], in_=X[:, c, :], func=AF.Square), reads=[Xr[c]], writes=[sq_r[i]])
                ssq_mm(i, c)
            finish_rstd()
            for c in range(KC):
                S.op("dve", lambda e, c=c: e.scalar_tensor_tensor(out=H[:, c, :], in0=X[:, c, :], scalar=PPt[:, gcol + c:gcol + c + 1], in1=rstd[:],
                                                                 op0=ALU.mult, op1=ALU.mult), reads=[Xr[c], r_pp, r_rstd], writes=[Hr[c]])

        def rnn_phase(with_y, cast_per=0):
            Bk = [dict() for _ in range(8)]

            def stA(nb):
                b = Bk[nb]
                par = nb % 2
                b["xc"] = []
                for j in range(2):
                    c = 2 * nb + j
                    pa, pr = proj("w_xr", c, H_rhs, H_res)
                    S.op("dve", lambda e, c=c, j=j: e.tensor_copy(out=XP[:, j, 0:3], in_=xhalo[:, c, 0:3]), reads=[xh_r[c]], writes=[XP_r[j]])
                    S.op("act", lambda e, j=j, pa=pa: e.activation(out=XP[:, j, 3:515], in_=pa[:], func=AF.Copy), reads=[pr], writes=[XP_r[j]])
                    S.op("dve", lambda e, c=c, j=j: e.tensor_copy(out=xhalo[:, c, 0:3], in_=XP[:, j, 512:515]), reads=[XP_r[j]], writes=[xh_r[c]])
                    xc, xcr = rsc()
                    cw = PC_CONVW + 4 * c
                    S.op("dve", lambda e, j=j, xc=xc, cw=cw, c=c: e.tensor_scalar(out=xc, in0=XP[:, j, 3:515], scalar1=PPt[:, cw + 3:cw + 4],
                                                                                 scalar2=PPt[:, PC_CONVB + c:PC_CONVB + c + 1], op0=ALU.mult, op1=ALU.add),
                         reads=[XP_r[j], r_pp], writes=[xcr])
                    for k in range(3):
                        S.op("dve", lambda e, j=j, xc=xc, cw=cw, k=k: e.scalar_tensor_tensor(out=xc, in0=XP[:, j, k:k + 512], scalar=PPt[:, cw + k:cw + k + 1],
                                                                                            in1=xc, op0=ALU.mult, op1=ALU.add),
                             reads=[XP_r[j], r_pp, xcr], writes=[xcr])
                    xi = 2 * par + j
                    S.op("pool", lambda e, xi=xi, xc=xc: e.tensor_copy(out=xcb[:, xi, :], in_=xc), reads=[xcr], writes=[xcb_r[xi]])
                    b["xc"].append((xc, xcr))

            def stB1a(nb):
                b = Bk[nb]
                par = nb % 2
                Wg, Wgr = wload("w_g", nb)
                grp = []
                for j in range(2):
                    for gi in range(2):
                        pa, pr = bank()
                        for kc in range(2):
                            o0 = gi * 512 + kc * 256 + j * 128
                            xi = 2 * par + kc
                            S.op("pe", lambda e, pa=pa, o0=o0, kc=kc, xi=xi, Wg=Wg: e.matmul(pa[:], lhsT=Wg[:, o0:o0 + 128], rhs=xcb[:, xi, :], start=(kc == 0), stop=(kc == 1)),
                                 reads=[Wgr, xcb_r[xi]], writes=[pr])
                        grp.append((j, gi, pa, pr))
                for j, gi, pa, pr in grp:
                    c = 2 * nb + j
                    bcol = PC_BA if gi == 0 else PC_BX
                    gt, gtr = rsc()
                    S.op("act", lambda e, pa=pa, gt=gt, bcol=bcol, c=c: e.activation(out=gt, in_=pa[:], func=AF.Sigmoid, bias=PPt[:, bcol + c:bcol + c + 1]),
                         reads=[pr, r_pp], writes=[gtr])
                    b[(j, gi)] = (gt, gtr)
                for j in range(2):
                    rt, rr = b[(j, 0)]
                    S.op("pool", lambda e, rt=rt: e.tensor_tensor(out=rt, in0=rt, in1=resetb[:], op=ALU.add), reads=[rr, r_resetb], writes=[rr])

            def stB1b(nb):
                b = Bk[nb]
                for j in range(2):
                    c = 2 * nb + j
                    rt, rr = b[(j, 0)]
                    at, ar = rsc()
                    mt, mr = rsc()
                    S.op("act", lambda e, rt=rt, at=at, c=c: e.activation(out=at, in_=rt, func=AF.Exp, scale=nsp8[:, c:c + 1]), reads=[rr, r_nsp], writes=[ar])
                    S.op("act", lambda e, rt=rt, mt=mt, c=c: e.activation(out=mt, in_=rt, func=AF.Exp, scale=nsp16[:, c:c + 1]), reads=[rr, r_nsp], writes=[mr])
                    b[("a", j)] = (at, ar)
                    b[("m", j)] = (mt, mr)
                for j in range(2):
                    mt, mr = b[("m", j)]
                    S.op("act", lambda e, mt=mt: e.activation(out=mt, in_=mt, func=AF.Sqrt, scale=-1.0, bias=1.0), reads=[mr], writes=[mr])
                for j in range(2):
                    it, ir = b[(j, 1)]
                    xc, xcr = b["xc"][j]
                    S.op("pool", lambda e, it=it, xc=xc: e.tensor_tensor(out=it, in0=it, in1=xc, op=ALU.mult), reads=[ir, xcr], writes=[ir])
                for j in range(2):
                    it, ir = b[(j, 1)]
                    mt, mr = b[("m", j)]
                    S.op("pool", lambda e, it=it, mt=mt: e.tensor_tensor(out=it, in0=it, in1=mt, op=ALU.mult), reads=[ir, mr], writes=[ir])

            def stB2(nb):
                b = Bk[nb]
                yp = []
                if with_y:
                    for j in range(2):
                        yp.append(proj("w_yr", 2 * nb + j, H_rhs, H_res))
                hts = []
                for j in range(2):
                    c = 2 * nb + j
                    at, ar = b[("a", j)]
                    it, ir = b[(j, 1)]
                    ht, hr = rsc()
                    S.op("dve", lambda e, at=at, it=it, ht=ht, c=c: e.tensor_tensor_scan(out=ht, data0=at, data1=it, initial=hstate[:, c:c + 1], op0=ALU.mult, op1=ALU.add),
                         reads=[ar, ir, hst_r[c]], writes=[hr])
                    S.op("dve", lambda e, ht=ht, c=c: e.tensor_copy(out=hstate[:, c:c + 1], in_=ht[:, T - 1:T]), reads=[hr], writes=[hst_r[c]])
                    hts.append((ht, hr))
                if with_y:
                    gys = []
                    for j in range(2):
                        pa, pr = yp[j]
                        gy, gyr = rsc()
                        S.op("act", lambda e, pa=pa, gy=gy: e.activation(out=gy, in_=pa[:], func=AF.Gelu_apprx_tanh), reads=[pr], writes=[gyr])
                        gys.append((gy, gyr))
                    for j in range(2):
                        c = 2 * nb + j
                        ht, hr = hts[j]
                        gy, gyr = gys[j]
                        S.op("pool", lambda e, ht=ht, gy=gy, c=c: e.tensor_tensor(out=R[:, c, :], in0=ht, in1=gy, op=ALU.mult), reads=[hr, gyr], writes=Rr[c])
                Bk[nb] = None
                cast_some(cast_per)

            if not with_y:
                for k in range(-3, 8):
                    if 0 <= k + 3 < 8:
                        stA(k + 3)
                    if 0 <= k + 2 < 8:
                        stB1a(k + 2)
                    if 0 <= k + 1 < 8:
                        stB1b(k + 1)
                    if 0 <= k < 8:
                        stB2(k)
            else:
                for k in range(-2, 8):
                    if 0 <= k + 2 < 8:
                        stA(k + 2)
                    if 0 <= k + 1 < 8:
                        stB1a(k + 1)
                    if 0 <= k < 8:
                        stB2(k)
                    if 0 <= k + 1 < 8:
                        stB1b(k + 1)

        def rope_a(pb):
            pa, pr = pb
            qb, qbr = qtile()
            S.op("act", lambda e: e.activation(out=qb, in_=pa[:], func=AF.Copy), reads=[pr], writes=[qbr])
            return pa, pr, qb, qbr

        def rope_b(stt, dst, dst_res):
            pa, pr, qb, qbr = stt
            pr2a, pr2r = bank()
            S.op("pe", lambda e: e.matmul(pr2a[:], lhsT=permb[:], rhs=qb, start=True, stop=True), reads=[qbr, r_perm], writes=[pr2r])
            t1, t1r = osc()
            t2, t2r = osc()
            S.op("dve", lambda e: e.tensor_tensor(out=t1, in0=pa[:], in1=cosT[:], op=ALU.mult), reads=[pr, r_cos], writes=[t1r])
            S.op("dve", lambda e: e.tensor_tensor(out=t2, in0=pr2a[:], in1=sinT[:], op=ALU.mult), reads=[pr2r, r_sin], writes=[t2r])
            S.op("pool", lambda e: e.tensor_tensor(out=dst, in0=t1, in1=t2, op=ALU.add), reads=[t1r, t2r], writes=dst_res)

        def produce_kv(with_q):
            for g in range(4):
                S.op("pool", lambda e, g=g: e.tensor_copy(out=Kr[:, g, 0:128], in_=Kr[:, g, 512:640]), reads=[Kr_r[g]], writes=[Krh_r[g]])
            S.op("pool", lambda e: e.tensor_copy(out=Vt[:, 0, :], in_=Vt[:, 4, :]), reads=[Vt_r[4]], writes=[Vt_r[0]])
            pbs = [bank() for _ in range(4)]
            for vq in range(4):
                W, Wr_ = wload("w_v", vq)
                for blk in range(4):
                    pa, pr = pbs[blk]
                    for kcl in range(4):
                        kc = vq * 4 + kcl
                        S.op("pe", lambda e, pa=pa, kc=kc, kcl=kcl, blk=blk, W=W: e.matmul(pa[:], lhsT=H[:, kc, blk * 128:(blk + 1) * 128], rhs=W[:, kcl * 512:(kcl + 1) * 512],
                                                                                         start=(kc == 0), stop=(kc == KC - 1)),
                             reads=[Wr_, Hr[kc]], writes=[pr])
            for blk in range(4):
                pa, pr = pbs[blk]
                S.op("act", lambda e, pa=pa, blk=blk: e.activation(out=Vt[:, blk + 1, :], in_=pa[:], func=AF.Copy), reads=[pr], writes=[Vt_r[blk + 1]])
            items = [("w_kd", g, Kr[:, g, 128:640], [Kr_r[g]]) for g in range(4)]
            if with_q:
                items += [("w_q", c, R[:, 16 + c, :], Rr[16 + c]) for c in range(KC)]
            prev = None
            for name, u, dst, dres in items:
                pb = proj(name, u, H_rhs, H_res)
                cur = (rope_a(pb), dst, dres)
                if prev is not None:
                    rope_b(*prev)
                prev = cur
            rope_b(*prev)

        def attention(first_sb):
            st["nbank"] = 8

            def stS(qb, g):
                qs = slice(qb * 128, (qb + 1) * 128)
                pts = {}
                for ch in range(2):
                    kcols = slice(qb * 128 + ch * 128, qb * 128 + ch * 128 + 128)
                    kres = [Krh_r[g]] if (ch == 0 and qb == 0) else [Kr_r[g]]
                    for half in range(2):
                        rows = slice(half * 64, half * 64 + 64)
                        pa, pr = bank()
                        qres = [Rr[16 + 4 * g + jj][qb] for jj in range(4)]
                        S.op("pe", lambda e, pa=pa, g=g, rows=rows, kcols=kcols, qs=qs: e.matmul(pa[:], lhsT=Kr[rows, g, kcols], rhs=R[rows, 16 + 4 * g:16 + 4 * g + 4, qs],
                                                                                             start=True, stop=True),
                             reads=kres + qres, writes=[pr])
                        pts[(ch, half)] = (pa, pr)
                for ch in range(2):
                    for half in range(2):
                        pa, pr = pts[(ch, half)]
                        pt, ptr = ptile()
                        S.op("act", lambda e, pa=pa, pt=pt: e.activation(out=pt, in_=pa[:], func=AF.Exp, scale=0.125), reads=[pr], writes=[ptr])
                        if ch == 0:
                            mk, mkr = (mprev0, r_mprev0) if (first_sb and qb == 0) else (mprev, r_mprev)
                        else:
                            mk, mkr = mown, r_mown
                        S.op("pool" if half == 0 else "dve", lambda e, pt=pt, mk=mk: e.tensor_tensor(out=pt.rearrange("p (h q) -> p h q", h=4), in0=pt.rearrange("p (h q) -> p h q", h=4),
                                                                            in1=mk[:].unsqueeze(1).to_broadcast([128, 4, 128]), op=ALU.mult),
                             reads=[ptr, mkr], writes=[ptr])
                        pts[(ch, half)] = (pt, ptr)
                return pts

            def stP(qb, g, pts):
                qs = slice(qb * 128, (qb + 1) * 128)
                pn, pnr = bank()
                pd, pdr = bank()
                for half in range(2):
                    rows = slice(half * 64, half * 64 + 64)
                    for ch in range(2):
                        pt, ptr = pts[(ch, half)]
                        blk = qb + ch
                        v0 = g * 128 + half * 64
                        S.op("pe", lambda e, rows=rows, blk=blk, v0=v0, pt=pt, ch=ch: e.matmul(pn[rows, :], lhsT=Vt[:, blk, v0:v0 + 64], rhs=pt, start=(ch == 0), stop=(ch == 1)),
                             reads=[Vt_r[blk], ptr], writes=[pnr])
                for half in range(2):
                    rows = slice(half * 64, half * 64 + 64)
                    for ch in range(2):
                        pt, ptr = pts[(ch, half)]
                        S.op("pe", lambda e, rows=rows, pt=pt, ch=ch, half=half: e.matmul(pd[rows, :], lhsT=onesb[:, half * 64:half * 64 + 64], rhs=pt, start=(ch == 0), stop=(ch == 1)),
                             reads=[r_ones, ptr], writes=[pdr])
                rc, rcr = osc()
                sc0 = g * 4
                if (qb * 4 + g) % 2 == 0:
                    for jj in range(4):
                        S.op("act", lambda e, rc=rc, jj=jj, sc0=sc0: e.activation(out=rc[:, jj * 128:(jj + 1) * 128], in_=pd[:, jj * 128:(jj + 1) * 128],
                                                                              func=AF.Ln, bias=esink[:, sc0 + jj:sc0 + jj + 1]),
                             reads=[pdr, r_esink], writes=[rcr])
                else:
                    S.op("dve", lambda e, rc=rc, sc0=sc0: e.tensor_tensor(out=rc.rearrange("p (h q) -> p h q", h=4),
                                                                        in0=pd[:].rearrange("p (h q) -> p h q", h=4),
                                                                        in1=esink[:, sc0:sc0 + 4].unsqueeze(2).to_broadcast([128, 4, 128]), op=ALU.add),
                         reads=[pdr, r_esink], writes=[rcr])
                    S.op("act", lambda e, rc=rc: e.activation(out=rc, in_=rc, func=AF.Ln), reads=[rcr], writes=[rcr])
                S.op("act", lambda e, rc=rc: e.activation(out=rc, in_=rc, func=AF.Exp, scale=-1.0), reads=[rcr], writes=[rcr])
                S.op("dve", lambda e, rc=rc, g=g, qs=qs: e.tensor_tensor(out=R[:, 16 + 4 * g:16 + 4 * g + 4, qs],
                                                                     in0=pn[:].rearrange("p (h q) -> p h q", h=4),
                                                                     in1=rc.rearrange("p (h q) -> p h q", h=4), op=ALU.mult),
                     reads=[pnr, rcr], writes=[Rr[16 + 4 * g + jj][qb] for jj in range(4)])

            prev = None
            for qb in range(4):
                for g in range(4):
                    pts = stS(qb, g)
                    if prev is not None:
                        stP(*prev)
                    prev = (qb, g, pts)
            stP(*prev)
            st["nbank"] = 7

        def mix_phase():
            YR_rhs = lambda kc: R[:, kc, :]
            YR_res = lambda kc: Rr[kc]
            YA_rhs = lambda kc: R[:, 16 + kc, :]
            YA_res = lambda kc: Rr[16 + kc]
            for n in range(KC):
                pg, pgr = proj("w_gr", n, H_rhs, H_res)
                s1, s1r = osc()
                S.op("act", lambda e, pg=pg, s1=s1: e.activation(out=s1, in_=pg[:], func=AF.Sigmoid), reads=[pgr], writes=[s1r])
                p1, p1r = proj("w_rnn", n, YR_rhs, YR_res)
                S.op("dve", lambda e, p1=p1, s1=s1: e.tensor_tensor(out=s1, in0=p1[:], in1=s1, op=ALU.mult), reads=[p1r, s1r], writes=[s1r])
                pg2, pg2r = proj("w_ga", n, H_rhs, H_res)
                s2, s2r = osc()
                S.op("act", lambda e, pg2=pg2, s2=s2: e.activation(out=s2, in_=pg2[:], func=AF.Sigmoid), reads=[pg2r], writes=[s2r])
                p2, p2r = proj("w_attn", n, YA_rhs, YA_res)
                S.op("dve", lambda e, p2=p2, s2=s2: e.tensor_tensor(out=s2, in0=p2[:], in1=s2, op=ALU.mult), reads=[p2r, s2r], writes=[s2r])
                S.op("pool", lambda e, s1=s1, s2=s2, n=n: e.tensor_tensor(out=R[:, 32 + n, :], in0=s1, in1=s2, op=ALU.add), reads=[s1r, s2r], writes=Rr[32 + n])

        def proj_to_O_with_ssq(groups):
            pend = None
            for n in range(KC):
                pa, pr = groups(n)
                if pend is not None:
                    ssq_mm(*pend)
                i = n % 2
                S.op("act", lambda e, pa=pa, n=n: e.activation(out=O[:, n, :], in_=pa[:], func=AF.Copy), reads=[pr], writes=[Or[n]])
                S.op("act", lambda e, pa=pa, i=i: e.activation(out=sq[:, i, :], in_=pa[:], func=AF.Square), reads=[pr], writes=[sq_r[i]])
                pend = (i, n)
            ssq_mm(*pend)
            finish_rstd()

        def out_phase():
            MX_rhs = lambda kc: R[:, 32 + kc, :]
            MX_res = lambda kc: Rr[32 + kc]
            proj_to_O_with_ssq(lambda n: proj("w_out", n, MX_rhs, MX_res))
            for n in range(KC):
                S.op("dve", lambda e, n=n: e.scalar_tensor_tensor(out=O[:, n, :], in0=O[:, n, :], scalar=PPt[:, PC_GPOST + n:PC_GPOST + n + 1], in1=rstd[:],
                                                                 op0=ALU.mult, op1=ALU.mult), reads=[Or[n], r_pp, r_rstd], writes=[Or[n]])
                S.op("dve" if n % 4 == 3 else "pool", lambda e, n=n: e.tensor_tensor(out=X[:, n, :], in0=O[:, n, :], in1=X[:, n, :], op=ALU.add), reads=[Or[n], Xr[n]], writes=[Xr[n]])

        def mlp_phase(t0, next_load=None):
            norm_X_to_H(PC_GMPRE)
            for j in range(64):
                pa, pr = proj("w_up", j, H_rhs, H_res)
                tt, ttr = osc()
                S.op("act", lambda e, pa=pa, tt=tt: e.activation(out=tt, in_=pa[:], func=AF.Relu), reads=[pr], writes=[ttr])
                S.op("pool", lambda e, tt=tt, j=j: e.tensor_tensor(out=R[:, j, :], in0=tt, in1=tt, op=ALU.mult), reads=[ttr], writes=Rr[j])

            def down_group(n):
                pb = bank()
                for kq in range(4):
                    proj("w_down", n * 4 + kq, lambda kc: R[:, kc, :], lambda kc: Rr[kc], koff=kq * 16, pb=pb, first=(kq == 0), last=(kq == 3))
                return pb

            proj_to_O_with_ssq(down_group)
            outs = []
            for n in range(KC):
                S.op("dve", lambda e, n=n: e.scalar_tensor_tensor(out=O[:, n, :], in0=O[:, n, :], scalar=PPt[:, PC_GMPOST + n:PC_GMPOST + n + 1], in1=rstd[:],
                                                                 op0=ALU.mult, op1=ALU.mult), reads=[Or[n], r_pp, r_rstd], writes=[Or[n]])
                S.op("dve" if n % 4 == 3 else "pool", lambda e, n=n: e.tensor_tensor(out=O[:, n, :], in0=O[:, n, :], in1=X[:, n, :], op=ALU.add), reads=[Or[n], Xr[n]], writes=[Or[n]])
                outs.append(S.op("sp", lambda e, n=n: e.dma_start(out=yT[n * 128:(n + 1) * 128, t0:t0 + T], in_=O[:, n, :]), reads=[Or[n]], dma=f"o{n}"))
                if next_load is not None and n % 4 == 3:
                    next_load(n // 4)
            return outs

        import os
        STAGE = int(os.environ.get("KSTAGE", "99"))
        for pbi in range(NPB if STAGE >= 1 else 0):
            t0 = pbi * T
            load_x(xpT, t0)
            load_pos(pospb, t0)
            norm_X_to_H(PC_GPRE)
            rnn_phase(False, cast_per=-(-(len(cast_list) - 32) // (NPB * 8)))
            if pbi == NPB - 1 and STAGE >= 2:
                rope_tables()
                produce_kv(False)
        S.op("dve", lambda e: e.tensor_scalar(out=hstate[:], in0=hstate[:], scalar1=PPt[:, PC_FLAG:PC_FLAG + 1], scalar2=None, op0=ALU.mult),
             reads=hst_r + [r_pp], writes=hst_r)
        S.op("dve", lambda e: e.tensor_scalar(out=xhalo[:], in0=xhalo[:], scalar1=PPt[:, PC_FLAG:PC_FLAG + 1], scalar2=None, op0=ALU.mult),
             reads=xh_r + [r_pp], writes=xh_r)

        all_outs = []
        for sbi in range(NSB if STAGE >= 3 else 0):
            t0 = sbi * T
            if sbi == 0 or STAGE < 8:
                load_x(xT, t0)
            load_pos(posb, t0, dma=(sbi == 0 or STAGE < 8))
            norm_X_to_H(PC_GPRE)
            rnn_phase(True, cast_per=(-(-(len(cast_list) - st["cast"]) // 8) if sbi == 0 else 0))
            cast_some(10 ** 9)
            if STAGE < 4:
                continue
            rope_tables()
            produce_kv(True)
            if STAGE < 5:
                continue
            attention(first_sb=(sbi == 0))
            if STAGE < 6:
                continue
            mix_phase()
            if STAGE < 7:
                continue
            out_phase()
            if STAGE < 8:
                continue
            def nl_(q, t1=t0 + T):
                load_x_group(xT, t1, q)
                if q == 3:
                    load_pos_dma(posb, t1)
            nl = nl_ if sbi + 1 < NSB else None
            all_outs.extend(mlp_phase(t0, nl))
        S.fence("sp", all_outs)
        S.emit(es)
    return nc


def _units(W):
    K, N = W.shape
    a = W.reshape(16, 128, N // 128, 128).transpose(2, 1, 0, 3)
    return np.ascontiguousarray(a).reshape(N // 128, 128, 2048)


def _prep_weights(inp):
    w_in = np.asarray(inp["w_in"][0])
    o = np.cumsum([0, 2048, 2048, 2048, 256, 256, 2048, 2048])
    xr, yr, q, k, v, gr, ga = [w_in[:, o[i]:o[i + 1]] for i in range(7)]
    kd = np.concatenate([np.concatenate([k[:, 64 * g:64 * g + 64]] * 2, axis=1) for g in range(4)], axis=1)
    vd = np.concatenate([np.concatenate([v[:, 64 * g:64 * g + 64]] * 2, axis=1) for g in range(4)], axis=1)
    vu = np.ascontiguousarray(vd.reshape(4, 4, 128, 512).transpose(0, 2, 1, 3)).reshape(4, 128, 2048)
    wa = np.asarray(inp["w_rg_a"][0]).reshape(8, 2, 128, 256).transpose(0, 2, 1, 3).reshape(8, 128, 512)
    wx = np.asarray(inp["w_rg_x"][0]).reshape(8, 2, 128, 256).transpose(0, 2, 1, 3).reshape(8, 128, 512)
    wg = np.ascontiguousarray(np.concatenate([wa, wx], axis=2))
    wd = np.asarray(inp["w_mlp_down"][0]).reshape(4, 16, 128, 16, 128).transpose(3, 0, 2, 1, 4)
    wd = np.ascontiguousarray(wd).reshape(64, 128, 2048)
    return {
        "w_xr": _units(xr), "w_yr": _units(yr), "w_q": _units(q), "w_kd": _units(kd), "w_v": vu,
        "w_gr": _units(gr), "w_ga": _units(ga), "w_g": wg,
        "w_rnn": _units(np.asarray(inp["w_rnn_proj"][0])), "w_attn": _units(np.asarray(inp["w_attn_proj"][0])),
        "w_out": _units(np.asarray(inp["w_out"][0])), "w_up": _units(np.asarray(inp["w_mlp_up"][0])), "w_down": wd,
    }


def _consts():
    cst = np.zeros((128, 384), np.float32)
    for m in range(128):
        d = m % 64
        if d < 32:
            cst[m + 32, m] = -1.0
        else:
            cst[m - 32, m] = 1.0
    kk = np.arange(128)[:, None]
    qq = np.arange(128)[None, :]
    cst[:, 128:256] = (kk > qq)
    cst[:, 256:384] = (kk <= qq)
    return cst


def _params(inp, flag):
    pp = np.zeros((128, NPP), np.float32)
    col = lambda v: np.asarray(v).reshape(16, 128).T
    pp[:, PC_GPRE:PC_GPRE + 16] = col(inp["norm_mix_pre"][0])
    pp[:, PC_GPOST:PC_GPOST + 16] = col(inp["norm_mix_post"][0])
    pp[:, PC_GMPRE:PC_GMPRE + 16] = col(inp["norm_mlp_pre"][0])
    pp[:, PC_GMPOST:PC_GMPOST + 16] = col(inp["norm_mlp_post"][0])
    cw = np.asarray(inp["conv_w"][0])
    pp[:, PC_CONVW:PC_CONVW + 64] = cw.reshape(4, 16, 128).transpose(2, 1, 0).reshape(128, 64)
    pp[:, PC_CONVB:PC_CONVB + 16] = col(inp["conv_b"][0])
    pp[:, PC_BA:PC_BA + 16] = col(inp["b_rg_a"][0])
    pp[:, PC_BX:PC_BX + 16] = col(inp["b_rg_x"][0])
    pp[:, PC_LAM:PC_LAM + 16] = col(inp["lru_lambda"][0])
    sk = np.asarray(inp["attn_sinks"][0]).reshape(4, 4, 2)
    pp[0:64, PC_SINK:PC_SINK + 16] = sk[:, :, 0].reshape(16)[None, :]
    pp[64:128, PC_SINK:PC_SINK + 16] = sk[:, :, 1].reshape(16)[None, :]
    invf = (10000.0 ** (-np.arange(0, 64, 2, dtype=np.float64) / 64.0)) / (2.0 * np.pi)
    pp[:, PC_INVF] = invf[(np.arange(128) % 64) % 32].astype(np.float32)
    pp[:, PC_FLAG] = flag
    return pp


_CACHE = {}


def kernel(**inputs):
    x = np.asarray(inputs["x"])
    positions = np.asarray(inputs["positions"])
    B, S_, _ = x.shape
    n_cores = 8
    per = n_cores // B
    TOK = S_ // per
    NSB = TOK // T
    key = (NSB,)
    if key not in _CACHE:
        _CACHE[key] = build_program(NSB, NSB)
    nc = _CACHE[key]
    wts = _prep_weights(inputs)
    cst = _consts()
    in_maps = []
    import os
    for c in range(int(os.environ.get("KCORES", n_cores))):
        b, half = c // per, c % per
        s0 = half * TOK
        m = {
            "xT": np.ascontiguousarray(x[b, s0:s0 + TOK].T),
            "xpT": np.ascontiguousarray(x[b, 0:TOK].T),
            "posb": np.ascontiguousarray(np.broadcast_to(positions[b, s0:s0 + TOK].astype(np.int32)[None], (128, TOK))),
            "pospb": np.ascontiguousarray(np.broadcast_to(positions[b, 0:TOK].astype(np.int32)[None], (128, TOK))),
            "pp": _params(inputs, float(half)),
            "cst": cst,
        }
        m.update(wts)
        in_maps.append(m)
    import os
    ncr = int(os.environ.get("KCORES", n_cores))
    res = run_bass_kernel_spmd(nc, in_maps[:ncr], core_ids=list(range(ncr)))
    out = np.zeros((B, S_, D), np.float32)
    for c in range(ncr):
        b, half = c // per, c % per
        s0 = half * TOK
        out[b, s0:s0 + TOK] = res.results[c]["yT"].T
    return out
```
